# Optimizing a Trainium2 kernel written in Bass

```python
import functools
import jax, jax.numpy as jnp
from jax import lax
import numpy as np

D_MODEL = 1024
BATCH = 16
SEQ = 2048
DEPTH = 1
DEC_BATCH = 128
DEC_SEQ = 8
PAST_LEN = 16384
PAGE_SIZE = 128

HEAD_DIM = 64
N_Q_HEADS = D_MODEL // 128
N_KV_HEADS = N_Q_HEADS // 4
GQA_GROUP = N_Q_HEADS // N_KV_HEADS
ATTN_W = N_Q_HEADS * HEAD_DIM
KV_W = N_KV_HEADS * HEAD_DIM
WINDOW = 128
RWKV_HEAD = 64
N_RWKV_HEADS = D_MODEL // 128
RWKV_W = N_RWKV_HEADS * RWKV_HEAD
DECAY_LORA = 64
AAA_LORA = 64
GATE_LORA = 128
RWKV_IN_W = 3 * RWKV_W + DECAY_LORA + AAA_LORA + GATE_LORA
GATE_W = 2 * D_MODEL
IN_W = ATTN_W + 2 * KV_W + RWKV_IN_W + GATE_W
D_FF = ((8 * D_MODEL // 3 + 127) // 128) * 128
CONV_W = 3
NORM_EPS = 1e-6
LNX_EPS = 64e-5

kernel_name = "hybrid_swa_sink_rwkv7_convffn_step"


def rmsnorm(x, g):
    xf = x.astype(jnp.float32)
    y = xf * lax.rsqrt(jnp.mean(xf * xf, axis=-1, keepdims=True) + NORM_EPS)
    return (y * g.astype(jnp.float32)).astype(x.dtype)


def softmax_with_sink(s, mask, sink):
    s = jnp.where(mask, s.astype(jnp.float32), -jnp.inf)
    sk = sink.astype(jnp.float32)[:, :, None, None]
    m = jnp.maximum(jnp.max(s, axis=-1, keepdims=True), sk)
    p = jnp.exp(s - m)
    return p / (jnp.sum(p, axis=-1, keepdims=True) + jnp.exp(sk - m))


def attn_prompt(q, k, v, sinks):
    B, T = q.shape[:2]
    nb = T // WINDOW
    qb = q.reshape(B, nb, WINDOW, N_KV_HEADS, GQA_GROUP, HEAD_DIM)
    kb = k.reshape(B, nb, WINDOW, N_KV_HEADS, HEAD_DIM)
    vb = v.reshape(B, nb, WINDOW, N_KV_HEADS, HEAD_DIM)
    pad = ((0, 0), (1, 0), (0, 0), (0, 0), (0, 0))
    kcat = jnp.concatenate([jnp.pad(kb[:, :-1], pad), kb], axis=2)
    vcat = jnp.concatenate([jnp.pad(vb[:, :-1], pad), vb], axis=2)
    s = jnp.einsum('bnqhgd,bnkhd->bnhgqk', qb, kcat) * (HEAD_DIM ** -0.5)
    i = np.arange(WINDOW)[:, None]
    c = np.arange(2 * WINDOW)[None, :]
    dist = WINDOW + i - c
    band = (dist >= 0) & (dist <= WINDOW)
    valid = (np.arange(nb)[:, None, None] > 0) | (c >= WINDOW)[None]
    mask = jnp.asarray(band[None] & valid)[None, :, None, None]
    p = softmax_with_sink(s, mask, sinks.reshape(N_KV_HEADS, GQA_GROUP))
    o = jnp.einsum('bnhgqk,bnkhd->bnqhgd', p.astype(v.dtype), vcat)
    return o.reshape(B, T, ATTN_W), k[:, -WINDOW:], v[:, -WINDOW:]


def attn_sample(q, k, v, cache_k, cache_v, sinks):
    B, S = q.shape[:2]
    kcat = jnp.concatenate([cache_k.astype(k.dtype), k], axis=1)
    vcat = jnp.concatenate([cache_v.astype(v.dtype), v], axis=1)
    qh = q.reshape(B, S, N_KV_HEADS, GQA_GROUP, HEAD_DIM)
    s = jnp.einsum('bqhgd,bkhd->bhgqk', qh, kcat) * (HEAD_DIM ** -0.5)
    i = np.arange(S)[:, None]
    c = np.arange(WINDOW + S)[None, :]
    dist = i + WINDOW - c
    mask = jnp.asarray((dist >= 0) & (dist <= WINDOW))
    p = softmax_with_sink(s, mask, sinks.reshape(N_KV_HEADS, GQA_GROUP))
    o = jnp.einsum('bhgqk,bkhd->bqhgd', p.astype(v.dtype), vcat)
    return o.reshape(B, S, ATTN_W), kcat[:, -WINDOW:], vcat[:, -WINDOW:]


def rwkv7_scan(S0, r, w, k, v, a, b):
    def step(S, inp):
        rt, wt, kt, vt, at, bt = inp
        sa = jnp.einsum('bhij,bhj->bhi', S, at)
        S = S * wt[:, :, None, :] + sa[..., None] * bt[:, :, None, :] + vt[..., None] * kt[:, :, None, :]
        return S, jnp.einsum('bhij,bhj->bhi', S, rt)
    xs = tuple(jnp.moveaxis(t.astype(jnp.float32), 1, 0) for t in (r, w, k, v, a, b))
    S, y = lax.scan(step, S0.astype(jnp.float32), xs)
    return jnp.moveaxis(y, 0, 1), S


def rwkv7_mixer(pb, shift_prev, S0, mu, w0, w_up, a0, a_up, g_up, k_k, k_a, r_k, lnx_w, lnx_b):
    B, T = pb.shape[:2]
    prev = jnp.concatenate([shift_prev[:, None].astype(pb.dtype), pb[:, :-1]], axis=1)
    xs = pb + (prev - pb) * mu
    o1, o2, o3 = RWKV_W, 2 * RWKV_W, 3 * RWKV_W
    o4, o5 = o3 + DECAY_LORA, o3 + DECAY_LORA + AAA_LORA
    r, k, v = xs[..., :o1], xs[..., o1:o2], xs[..., o2:o3]
    wd, ad, gd = xs[..., o3:o4], xs[..., o4:o5], xs[..., o5:]
    w = -jax.nn.softplus(-(w0 + jnp.tanh(wd) @ w_up)) - 0.5
    decay = jnp.exp(-jnp.exp(w.astype(jnp.float32)))
    a = jax.nn.sigmoid(a0 + ad @ a_up)
    g = jax.nn.sigmoid(gd) @ g_up
    hs = (B, T, N_RWKV_HEADS, RWKV_HEAD)
    kk = (k * k_k).reshape(hs).astype(jnp.float32)
    kk = kk / jnp.maximum(jnp.sqrt(jnp.sum(kk * kk, axis=-1, keepdims=True)), 1e-12)
    k = k * (1 + (a - 1) * k_a)
    rh, kh, vh, ah = r.reshape(hs), k.reshape(hs), v.reshape(hs), a.reshape(hs)
    y, S = rwkv7_scan(S0, rh, decay.reshape(hs), kh, vh, -kk, kk * ah.astype(jnp.float32))
    mean = jnp.mean(y, axis=-1, keepdims=True)
    var = jnp.mean(jnp.square(y - mean), axis=-1, keepdims=True)
    y = ((y - mean) * lax.rsqrt(var + LNX_EPS)).reshape(B, T, RWKV_W)
    y = y * lnx_w.astype(jnp.float32) + lnx_b.astype(jnp.float32)
    bonus = jnp.sum((rh * kh * r_k).astype(jnp.float32), axis=-1, keepdims=True) * vh.astype(jnp.float32)
    y = (y + bonus.reshape(B, T, RWKV_W)).astype(pb.dtype) * g
    return y, pb[:, -1], S


def conv_ffn(h, conv_prev, w_in, cw, cb, w_out):
    T = h.shape[1]
    u = h @ w_in
    gp, vp = u[..., :D_FF], u[..., D_FF:]
    buf = jnp.concatenate([conv_prev.astype(gp.dtype), gp], axis=1)
    c = cb + sum(cw[j] * buf[:, j:j + T] for j in range(CONV_W))
    return (jax.nn.gelu(c, approximate=True) * vp) @ w_out, buf[:, -(CONV_W - 1):]


def trunk_layer(x, attention, shift_prev, wkv0, conv_prev, norm_attn, w_in, b_gate,
                mu_shift, w0, w_lora_up, a0, a_lora_up, g_lora_up, k_k, k_a, r_k, lnx_w, lnx_b,
                w_branch_attn, w_branch_rwkv, w_out, norm_ffn, w_ffn_in, ffn_conv_w, ffn_conv_b, w_ffn_out):
    B, T = x.shape[:2]
    h = rmsnorm(x, norm_attn)
    p = h @ w_in
    e1, e2, e3, e4 = ATTN_W, ATTN_W + KV_W, ATTN_W + 2 * KV_W, ATTN_W + 2 * KV_W + RWKV_IN_W
    q = p[..., :e1].reshape(B, T, N_Q_HEADS, HEAD_DIM)
    k = p[..., e1:e2].reshape(B, T, N_KV_HEADS, HEAD_DIM)
    v = p[..., e2:e3].reshape(B, T, N_KV_HEADS, HEAD_DIM)
    o_attn, win_k, win_v = attention(q, k, v)
    o_rwkv, shift_new, wkv_new = rwkv7_mixer(p[..., e3:e4], shift_prev, wkv0, mu_shift, w0, w_lora_up,
                                             a0, a_lora_up, g_lora_up, k_k, k_a, r_k, lnx_w, lnx_b)
    gates = jax.nn.sigmoid(p[..., e4:] + b_gate)
    merged = gates[..., :D_MODEL] * (o_attn @ w_branch_attn) + gates[..., D_MODEL:] * (o_rwkv @ w_branch_rwkv)
    x = x + merged @ w_out
    f, conv_new = conv_ffn(rmsnorm(x, norm_ffn), conv_prev, w_ffn_in, ffn_conv_w, ffn_conv_b, w_ffn_out)
    return x + f, win_k, win_v, shift_new, wkv_new, conv_new


def setup_inputs(seed: int = 0) -> dict:
    key = jax.random.key(seed)
    ks = jax.random.split(key, 32)
    f32 = jnp.float32

    def nrm(k, shape, scale):
        return jax.random.normal(k, shape, f32) * scale

    L = DEPTH
    return {
        "x_prompt": nrm(ks[0], (BATCH, SEQ, D_MODEL), 1.0),
        "x_sample": nrm(ks[1], (DEC_BATCH, DEC_SEQ, D_MODEL), 1.0),
        "cache_win_k": nrm(ks[2], (L, DEC_BATCH, WINDOW, N_KV_HEADS, HEAD_DIM), 1.0),
        "cache_win_v": nrm(ks[3], (L, DEC_BATCH, WINDOW, N_KV_HEADS, HEAD_DIM), 1.0),
        "state_shift": nrm(ks[4], (L, DEC_BATCH, RWKV_IN_W), 1.0),
        "state_wkv": nrm(ks[5], (L, DEC_BATCH, N_RWKV_HEADS, RWKV_HEAD, RWKV_HEAD), 0.5),
        "state_ffn_conv": nrm(ks[6], (L, DEC_BATCH, CONV_W - 1, D_FF), 1.0),
        "norm_attn": 1.0 + nrm(ks[7], (L, D_MODEL), 0.02),
        "w_in": nrm(ks[8], (L, D_MODEL, IN_W), D_MODEL ** -0.5),
        "b_gate": nrm(ks[9], (L, GATE_W), 0.02),
        "attn_sinks": nrm(ks[10], (L, N_Q_HEADS), 0.5),
        "mu_shift": jax.random.uniform(ks[11], (L, RWKV_IN_W), f32),
        "w0": jax.random.uniform(ks[12], (L, RWKV_W), f32, -4.0, 1.0),
        "w_lora_up": nrm(ks[13], (L, DECAY_LORA, RWKV_W), 0.1 * DECAY_LORA ** -0.5),
        "a0": nrm(ks[14], (L, RWKV_W), 0.1),
        "a_lora_up": nrm(ks[15], (L, AAA_LORA, RWKV_W), 0.1 * AAA_LORA ** -0.5),
        "g_lora_up": nrm(ks[16], (L, GATE_LORA, RWKV_W), GATE_LORA ** -0.5),
        "k_k": 0.85 + nrm(ks[17], (L, RWKV_W), 0.02),
        "k_a": 1.0 + nrm(ks[18], (L, RWKV_W), 0.02),
        "r_k": nrm(ks[19], (L, N_RWKV_HEADS, RWKV_HEAD), 0.1),
        "lnx_w": 1.0 + nrm(ks[20], (L, RWKV_W), 0.02),
        "lnx_b": nrm(ks[21], (L, RWKV_W), 0.02),
        "w_branch_attn": nrm(ks[22], (L, ATTN_W, D_MODEL), ATTN_W ** -0.5),
        "w_branch_rwkv": nrm(ks[23], (L, RWKV_W, D_MODEL), RWKV_W ** -0.5),
        "w_out": nrm(ks[24], (L, D_MODEL, D_MODEL), D_MODEL ** -0.5),
        "norm_ffn": 1.0 + nrm(ks[25], (L, D_MODEL), 0.02),
        "w_ffn_in": nrm(ks[26], (L, D_MODEL, 2 * D_FF), D_MODEL ** -0.5),
        "ffn_conv_w": nrm(ks[27], (L, CONV_W, D_FF), CONV_W ** -0.5),
        "ffn_conv_b": nrm(ks[28], (L, D_FF), 0.02),
        "w_ffn_out": nrm(ks[29], (L, D_FF, D_MODEL), D_FF ** -0.5),
        "norm_final": 1.0 + nrm(ks[30], (D_MODEL,), 0.02),
    }


def reference(x_prompt, x_sample, cache_win_k, cache_win_v, state_shift, state_wkv, state_ffn_conv,
              norm_attn, w_in, b_gate, attn_sinks, mu_shift, w0, w_lora_up, a0, a_lora_up, g_lora_up,
              k_k, k_a, r_k, lnx_w, lnx_b, w_branch_attn, w_branch_rwkv, w_out, norm_ffn, w_ffn_in,
              ffn_conv_w, ffn_conv_b, w_ffn_out, norm_final):
    xp, xs = x_prompt, x_sample
    pk, pv, psh, pwkv, pcv = [], [], [], [], []
    sk, sv, ssh, swkv, scv = [], [], [], [], []
    for l in range(DEPTH):
        weights = (norm_attn[l], w_in[l], b_gate[l], mu_shift[l], w0[l], w_lora_up[l], a0[l], a_lora_up[l],
                   g_lora_up[l], k_k[l], k_a[l], r_k[l], lnx_w[l], lnx_b[l], w_branch_attn[l],
                   w_branch_rwkv[l], w_out[l], norm_ffn[l], w_ffn_in[l], ffn_conv_w[l], ffn_conv_b[l], w_ffn_out[l])
        xp, a1, a2, a3, a4, a5 = trunk_layer(
            xp, functools.partial(attn_prompt, sinks=attn_sinks[l]),
            jnp.zeros((xp.shape[0], RWKV_IN_W), xp.dtype),
            jnp.zeros((xp.shape[0], N_RWKV_HEADS, RWKV_HEAD, RWKV_HEAD), jnp.float32),
            jnp.zeros((xp.shape[0], CONV_W - 1, D_FF), xp.dtype), *weights)
        pk.append(a1); pv.append(a2); psh.append(a3); pwkv.append(a4); pcv.append(a5)
        xs, b1, b2, b3, b4, b5 = trunk_layer(
            xs, functools.partial(attn_sample, cache_k=cache_win_k[l], cache_v=cache_win_v[l], sinks=attn_sinks[l]),
            state_shift[l], state_wkv[l], state_ffn_conv[l], *weights)
        sk.append(b1); sv.append(b2); ssh.append(b3); swkv.append(b4); scv.append(b5)
    y_prompt = rmsnorm(xp, norm_final)
    y_sample = rmsnorm(xs, norm_final)
    return (y_prompt, y_sample,
            jnp.stack(pk), jnp.stack(pv), jnp.stack(psh), jnp.stack(pwkv), jnp.stack(pcv),
            jnp.stack(sk), jnp.stack(sv), jnp.stack(ssh), jnp.stack(swkv), jnp.stack(scv))
```

```python
import numpy as np
import concourse.bass as bass
import concourse.mybir as mybir
from concourse.ap import AP
from concourse.bass_utils import run_bass_kernel_spmd

F32 = mybir.dt.float32
BF16 = mybir.dt.bfloat16
ALU = mybir.AluOpType
AF = mybir.ActivationFunctionType
AX = mybir.AxisListType

ENGS = ("pe", "act", "dve", "pool", "sp")
EPOCH = 12000

D = 1024
HD = 64
NQH = 8
NKV = 2
RW = 512
RIN = 1792
DFF = 2816
NFC = 22
NB = 32
NORM_EPS = 1e-6
LNX_EPS = 64e-5
NEG = -30000.0


class T:
    def __init__(self, name, h=None):
        self.name = name
        self.h = h
        self.last_w = None
        self.readers = []
        self.dsem = None
        self.dcount = 0
        self.overlaps = []

    def __getitem__(self, k):
        return self.h[k]


class Prog:
    def __init__(self, nc):
        self.nc = nc
        self.q = {e: [] for e in ENGS}
        self.n = {e: 0 for e in ENGS}
        self.sems = {}
        self.water = {e: {} for e in ENGS}
        self.nsem = 0
        self.psum_free = []
        self.psum_all = []
        self.uid = 0
        self.outdeps = []
        self.rec = None
        self.rec_banks = []
        self.task_banks = {}

    def sem(self, key):
        if key not in self.sems:
            self.sems[key] = self.nc.alloc_semaphore("s%d" % self.nsem)
            self.nsem += 1
        return self.sems[key]

    def dram(self, name, shape, dt, kind="Internal"):
        t = T(name, None)
        t.d = self.nc.dram_tensor(name, list(shape), dt, kind=kind)
        t.ap = t.d.ap()
        t.is_out = (kind == "ExternalOutput")
        return t

    def psum_init(self):
        for i in range(8):
            t = T("ps%d" % i, self.nc.alloc_psum_tensor("ps%d" % i, [128, 512], F32))
            t.psum = True
            self.psum_all.append(t)
        self.psum_free = list(self.psum_all)

    def psum(self):
        assert self.psum_free, "out of PSUM banks"
        t = self.psum_free.pop(0)
        if self.rec is not None:
            self.rec_banks.append([t.name, len(self.rec), None])
        return t

    def pfree(self, t):
        assert t not in self.psum_free
        self.psum_free.append(t)
        if self.rec is not None:
            for e in self.rec_banks:
                if e[0] == t.name and e[2] is None:
                    e[2] = len(self.rec)

    def _deps(self, eng, reads, writes):
        deps = []
        for t in reads:
            if t.last_w is not None:
                deps.append(t.last_w)
            if getattr(t, "psum", False):
                deps.extend(r for r in t.readers if r[0][0] != eng)
        for t in writes:
            for tt in [t] + t.overlaps:
                if tt.last_w is not None:
                    deps.append(tt.last_w)
                deps.extend(tt.readers)
        best = {}
        for (k, v) in deps:
            if eng == "pe" and k[0] == "pe":
                continue
            if v > best.get(k, 0):
                best[k] = v
        out = []
        for k, v in best.items():
            if self.water[eng].get(k, 0) >= v:
                continue
            self.water[eng][k] = v
            out.append((k, v))
        return out

    def _commit(self, dep, reads, writes):
        for t in reads:
            t.readers.append(dep)
            if len(t.readers) > 48:
                best = {}
                for k, v in t.readers:
                    if v > best.get(k, 0):
                        best[k] = v
                t.readers = list(best.items())
        for t in writes:
            t.last_w = dep
            t.readers = []
            t.overlaps = []

    def begin_task(self):
        assert self.rec is None
        self.rec = []
        self.rec_banks = []

    def end_task(self):
        r, self.rec = self.rec, None
        self.task_banks[id(r)] = self.rec_banks
        return r

    def interleave(self, tasks):
        uses = []
        for k, t in enumerate(tasks):
            for (nm, i0, i1) in self.task_banks.get(id(t), []):
                uses.append((nm, i0, len(t) if i1 is None else i1, k))
        for x in uses:
            for y in uses:
                if x[0] == y[0] and x[3] < y[3]:
                    assert x[2] <= y[1] or y[2] <= x[1], ("psum bank overlap between interleaved tasks", x, y)
        idx = [0] * len(tasks)
        left = sum(len(t) for t in tasks)
        while left:
            for k, t in enumerate(tasks):
                if idx[k] < len(t):
                    kind, args = t[idx[k]]
                    idx[k] += 1
                    left -= 1
                    if kind == "op":
                        self.op(*args)
                    else:
                        self.dma(*args[:-1], final=args[-1])

    def op(self, eng, fn, reads=(), writes=()):
        if self.rec is not None:
            self.rec.append(("op", (eng, fn, list(reads), list(writes))))
            return None
        reads = [t for t in reads if t is not None]
        writes = [t for t in writes if t is not None]
        waits = self._deps(eng, reads, writes)
        idx = self.n[eng]
        self.n[eng] += 1
        key = (eng, idx // EPOCH)
        val = idx % EPOCH + 1
        self.sem(key)
        self.q[eng].append((waits, fn, key, 1))
        dep = (key, val)
        self._commit(dep, reads, writes)
        return dep

    def dma(self, eng, out_ap, in_ap, sbt, reads=(), writes=(), final=False):
        if self.rec is not None:
            self.rec.append(("dma", (eng, out_ap, in_ap, sbt, list(reads), list(writes), final)))
            return None
        reads = [t for t in reads if t is not None]
        writes = [t for t in writes if t is not None and not (final and getattr(t, "is_out", False))]
        waits = self._deps(eng, reads, writes)
        if sbt.dsem is None:
            self.uid += 1
            sbt.dsem = ("d", "%s_%d" % (sbt.name, self.uid))
            self.sem(sbt.dsem)
        sbt.dcount += 1
        key = sbt.dsem
        val = 16 * sbt.dcount

        def fn(e, out_ap=out_ap, in_ap=in_ap):
            return e.dma_start(out=out_ap, in_=in_ap)
        self.q[eng].append((waits, fn, key, 16))
        dep = (key, val)
        self._commit(dep, reads, writes)
        if final:
            self.outdeps.append(dep)
        return dep

    def emit(self, final_deps):
        nc = self.nc
        fin = {}
        for k, v in final_deps:
            if v > fin.get(k, 0):
                fin[k] = v
        P = self

        def run(eng_name):
            def body(e):
                for waits, fn, key, inc in P.q[eng_name]:
                    for k, v in waits:
                        e.wait_ge(P.sems[k], v)
                    ins = fn(e)
                    ins.then_inc(P.sems[key], inc)
                if eng_name == "sp":
                    for k, v in fin.items():
                        e.wait_ge(P.sems[k], v)
            return body

        with nc.Block() as block:
            block.sync(run("sp"))
            block.tensor(run("pe"))
            block.scalar(run("act"))
            block.vector(run("dve"))
            block.gpsimd(run("pool"))


class StopBuild(Exception):
    pass


class Arena:
    def __init__(self, P, words):
        self.P = P
        self.words = words
        self.h = P.nc.alloc_sbuf_tensor("arena", [128, words], F32)
        self.top = 0
        self.ghosts = []
        self.live = []
        self.peak = 0

    def alloc(self, name, shape, dt=F32):
        n = int(np.prod(shape))
        w = n if dt == F32 else (n + 1) // 2
        w = (w + 7) // 8 * 8
        s, e = self.top, self.top + w
        if e > self.words:
            print("ARENA OVERFLOW", name, e, [(t.name, (b - a) * 4) for a, b, t in self.live if (b - a) * 4 >= 2000])
        assert e <= self.words, "arena overflow %s %d" % (name, e)
        self.top = e
        self.peak = max(self.peak, e)
        ap = self.h[:, s:e]
        if dt != F32:
            ap = ap.bitcast(dt)
        ap = ap[:, 0:n]
        if len(shape) == 2:
            ap = ap.rearrange("p (a b) -> p a b", b=shape[1])
        elif len(shape) == 3:
            ap = ap.rearrange("p (a b c) -> p a b c", b=shape[1], c=shape[2])
        elif len(shape) == 4:
            ap = ap.rearrange("p (a b c d) -> p a b c d", b=shape[1], c=shape[2], d=shape[3])
        t = T(name, ap)
        ov = []
        keep = []
        for (gs, ge, gt) in self.ghosts:
            if gs < e and s < ge:
                ov.append(gt)
                if not (s <= gs and ge <= e):
                    keep.append((gs, ge, gt))
            else:
                keep.append((gs, ge, gt))
        self.ghosts = keep
        t.overlaps = ov
        self.live.append((s, e, t))
        return t

    def mark(self):
        return (self.top, len(self.live))

    def release(self, m):
        top, nl = m
        for g in self.live[nl:]:
            self.ghosts.append(g)
        self.live = self.live[:nl]
        self.top = top


def bc(ap_tensor, offset, pairs):
    return AP(ap_tensor, offset, [list(p) for p in pairs])


def pack_weights(w_in, wba, wbr, w_out, w_fin, w_fout):
    ws = np.zeros((NB, 128, 4096), np.float32)

    def kc_block(W, c0, ncols):
        blk = np.zeros((128, 8, 512), np.float32)
        blk[:, :, :ncols] = W[:, c0:c0 + ncols].reshape(8, 128, ncols).transpose(1, 0, 2)
        return blk.reshape(128, 4096)
    ws[0] = kc_block(w_in, 0, 512)
    ws[1] = kc_block(w_in, 512, 256)
    for i in range(3):
        ws[2 + i] = kc_block(w_in, 768 + 512 * i, 512)
    ws[5] = kc_block(w_in, 768 + 1536, 256)
    for i in range(4):
        ws[6 + i] = kc_block(w_in, 2560 + 512 * i, 512)
    for i in range(2):
        blk = np.zeros((128, 8, 512), np.float32)
        blk[0:64] = wba[:, i * 512:(i + 1) * 512].reshape(8, 64, 512).transpose(1, 0, 2)
        ws[10 + i] = blk.reshape(128, 4096)
    ws[12] = wbr.reshape(4, 128, 1024).transpose(1, 0, 2).reshape(128, 4096)
    for i in range(2):
        ws[13 + i] = kc_block(w_out, 512 * i, 512)
    for i in range(11):
        blk = np.zeros((128, 8, 512), np.float32)
        Wr = w_fin.reshape(8, 128, 2 * DFF).transpose(1, 0, 2)
        blk[:, :, 0:256] = Wr[:, :, i * 256:(i + 1) * 256]
        blk[:, :, 256:512] = Wr[:, :, DFF + i * 256:DFF + (i + 1) * 256]
        ws[15 + i] = blk.reshape(128, 4096)
    for hf in range(2):
        for gi in range(3):
            ncc = 8 if gi < 2 else 6
            blk = np.zeros((128, 8, 512), np.float32)
            rows = w_fout[gi * 1024:gi * 1024 + ncc * 128, hf * 512:(hf + 1) * 512]
            blk[:, :ncc, :] = rows.reshape(ncc, 128, 512).transpose(1, 0, 2)
            ws[26 + hf * 3 + gi] = blk.reshape(128, 4096)
    return ws


def fm(v, nch):
    return np.ascontiguousarray(v.reshape(nch, 128).T)


def col_layout():
    names = [("g_attn", 8), ("g_ffn", 8), ("mu", 14), ("bgate", 16), ("w0", 4), ("a0", 4), ("kk", 4), ("ka", 4),
             ("rk", 4), ("lnw", 4), ("lnb", 4), ("cb", 22), ("cw0", 22), ("cw1", 22), ("cw2", 22)]
    off = {}
    o = 0
    for n, w in names:
        off[n] = (o, w)
        o += w
    return off, o


COLS, NCOLS = col_layout()


def const_layout():
    names = [("ident", 128), ("maskA", 256), ("maskA0", 256), ("maskS", 256),
             ("mUs", 128), ("mUi", 128), ("mUs2", 128), ("mUi2", 128), ("mLs", 128),
             ("smUs", 128), ("smUi", 128), ("smUs2", 128), ("smUi2", 128), ("smLs", 128),
             ("blockm", 128), ("seqm", 16), ("lastm", 16), ("segP", 256), ("segS", 128), ("gfin", 1024), ("sink", 8), ("sinkS", 2)]
    off = {}
    o = 0
    for n, w in names:
        off[n] = (o, w)
        o += w
    return off, o


CONSTS, NCONST = const_layout()


def make_consts(norm_final, sinks):
    c = np.zeros((128, NCONST), np.float32)

    def put(n, a):
        o, w = CONSTS[n]
        c[:, o:o + w] = a
    i = np.arange(128)[:, None]
    cc = np.arange(256)[None, :]
    dist = 128 + i - cc
    band = (dist >= 0) & (dist <= 128)
    put("ident", np.eye(128, dtype=np.float32))
    put("maskA", np.where(band, 0.0, NEG))
    put("maskA0", np.where(band & (cc >= 128), 0.0, NEG))
    ms = np.full((128, 256), NEG, np.float32)
    ms[0:32] = np.tile(np.where(band, 0.0, NEG)[0:8], (4, 1))
    put("maskS", ms)
    s = np.arange(128)[:, None]
    t = np.arange(128)[None, :]
    mus = (s < t).astype(np.float32)
    mui = (s <= t).astype(np.float32)
    mls = (t < s).astype(np.float32)
    put("mUs", mus); put("mUi", mui); put("mUs2", mus); put("mUi2", mui); put("mLs", mls)
    same = ((s // 8) == (t // 8)).astype(np.float32)
    put("smUs", mus * same); put("smUi", mui * same); put("smUs2", mus * same); put("smUi2", mui * same)
    put("smLs", mls * same)
    put("blockm", ((s // 64) == (t // 64)).astype(np.float32))
    sq = np.arange(16)[None, :]
    put("seqm", ((s // 8) == sq).astype(np.float32))
    put("lastm", (s == sq * 8 + 7).astype(np.float32))
    tt = np.arange(256)[None, :]
    put("segP", np.broadcast_to((tt % 128 != 0).astype(np.float32), (128, 256)))
    t8 = np.arange(128)[None, :]
    put("segS", np.broadcast_to((t8 % 8 != 0).astype(np.float32), (128, 128)))
    put("gfin", np.broadcast_to(norm_final[None, :], (128, 1024)))
    put("sink", np.broadcast_to(sinks[None, :], (128, 8)))
    sS = np.zeros((128, 2), np.float32)
    for kvh in range(2):
        for g in range(4):
            sS[g * 8:(g + 1) * 8, kvh] = sinks[kvh * 4 + g]
    put("sinkS", sS)
    return c


def build(cfg):
    NSEQ = cfg["nseq"]
    SEQ = cfg["seq"]
    TP = cfg["tp"]
    HAS_S = cfg.get("sample", True)
    DBG = cfg.get("dbg", None)
    nmt = SEQ // TP

    nc = bass.Bass("TRN2", target_bir_lowering=False)
    P = Prog(nc)
    P.psum_init()
    A = Arena(P, cfg.get("arena_words", 53000))

    xp_d = P.dram("xp", [NSEQ, SEQ, D], F32, "ExternalInput")
    ws_d = P.dram("wstream", [NB, 128, 4096], F32, "ExternalInput")
    cols_d = P.dram("cols", [128, NCOLS], F32, "ExternalInput")
    const_d = P.dram("consts", [128, NCONST], F32, "ExternalInput")
    lora_d = P.dram("lora", [128, 1024], F32, "ExternalInput")
    yp_d = P.dram("yp", [NSEQ, SEQ, D], F32, "ExternalOutput")
    wkp_d = P.dram("wkp", [NSEQ, 128, 128], F32, "ExternalOutput")
    wvp_d = P.dram("wvp", [NSEQ, 128, 128], F32, "ExternalOutput")
    shp_d = P.dram("shp", [NSEQ, RIN], F32, "ExternalOutput")
    wkvp_d = P.dram("wkvp", [NSEQ, 8, 64, 64], F32, "ExternalOutput")
    cvp_d = P.dram("cvp", [NSEQ, 2, DFF], F32, "ExternalOutput")
    if HAS_S:
        xs_d = P.dram("xsm", [128, D], F32, "ExternalInput")
        ck_d = P.dram("ck", [16, 128, 128], F32, "ExternalInput")
        cv_d = P.dram("cv", [16, 128, 128], F32, "ExternalInput")
        ssh_d = P.dram("ssh", [16, RIN], F32, "ExternalInput")
        swkv_d = P.dram("swkv", [16, 8, 64, 64], F32, "ExternalInput")
        scv_d = P.dram("scv", [32, DFF], F32, "ExternalInput")
        ys_d = P.dram("ys", [128, D], F32, "ExternalOutput")
        wks_d = P.dram("wks", [16, 128, 128], F32, "ExternalOutput")
        wvs_d = P.dram("wvs", [16, 128, 128], F32, "ExternalOutput")
        shs_d = P.dram("shs", [16, RIN], F32, "ExternalOutput")
        wkvs_d = P.dram("wkvs", [16, 8, 64, 64], F32, "ExternalOutput")
        cvs_d = P.dram("cvs", [32, DFF], F32, "ExternalOutput")
    wbf = [P.dram("wbf%d" % g, [4, 128, 4096], BF16) for g in range(8)]
    outs = []
    dbg_out = {}

    def tt(eng, out, in0, in1, op, R, W):
        return P.op(eng, lambda e: e.tensor_tensor(out=out, in0=in0, in1=in1, op=op), R, W)

    def ts(eng, out, in0, s1, s2, op0, op1, R, W):
        if s2 is None:
            return P.op(eng, lambda e: e.tensor_scalar(out=out, in0=in0, scalar1=s1, scalar2=None, op0=op0), R, W)
        return P.op(eng, lambda e: e.tensor_scalar(out=out, in0=in0, scalar1=s1, scalar2=s2, op0=op0, op1=op1), R, W)

    def stt(eng, out, in0, sc, in1, op0, op1, R, W):
        return P.op(eng, lambda e: e.scalar_tensor_tensor(out=out, in0=in0, scalar=sc, in1=in1, op0=op0, op1=op1), R, W)

    def act(out, in_, func, R, W, bias=None, scale=None, accum=None):
        kw = {}
        if bias is not None:
            kw["bias"] = bias
        if scale is not None:
            kw["scale"] = scale
        if accum is not None:
            kw["accum_out"] = accum
        return P.op("act", lambda e: e.activation(out=out, in_=in_, func=func, **kw), R, W)

    def cp(eng, out, in_, R, W):
        if eng == "act":
            return P.op("act", lambda e: e.copy(out=out, in_=in_), R, W)
        return P.op(eng, lambda e: e.tensor_copy(out=out, in_=in_), R, W)

    def mm(out, lhsT, rhs, start, stop, R, W):
        return P.op("pe", lambda e: e.matmul(out, lhsT, rhs, start=start, stop=stop, skip_group_check=True), R, W)

    def tr(out, in_, ident, R, W):
        return P.op("pe", lambda e: e.transpose(out, in_, ident), R, W)

    def red(eng, out, in_, op, R, W):
        return P.op(eng, lambda e: e.tensor_reduce(out=out, in_=in_, axis=AX.X, op=op), R, W)

    def recip(out, in_, R, W):
        return P.op("dve", lambda e: e.reciprocal(out=out, in_=in_), R, W)

    def mset(eng, ap, v, W):
        return P.op(eng, lambda e: e.memset(ap, v), [], W)

    def dump(name, t, ap, shape):
        if DBG and name in DBG and name not in dbg_out:
            d = P.dram("dbg_" + name, list(shape), ap.dtype, "ExternalOutput")
            P.dma("sp", d.ap, ap, t, reads=[t], writes=[d], final=True)
            dbg_out[name] = d
            outs.append(d)

    cst = A.alloc("cst", [NCONST])
    colt = A.alloc("colt", [NCOLS])
    omm = A.alloc("omm", [14])
    omka = A.alloc("omka", [4])
    nsink = A.alloc("nsink", [8])
    identb = A.alloc("identb", [128], BF16)
    blockb = A.alloc("blockb", [128], BF16)
    lorab = A.alloc("lorab", [1024], BF16)
    ones64 = A.alloc("ones64", [64])
    NSLOT = 4
    slots = [A.alloc("wslot%d" % i, [4096], BF16) for i in range(NSLOT)]
    xts = [A.alloc("xt%d" % i, [TP // 128, D]) for i in range(2)]
    Sf = A.alloc("Sf", [4, 128])
    Sb = A.alloc("Sb", [4, 128], BF16)
    shc = A.alloc("shc", [14])
    cvc = A.alloc("cvc", [NFC, 2])
    kT = A.alloc("kT", [2, 128 + TP], BF16)
    vall = A.alloc("vall", [1 + TP // 128, 128], BF16)
    ystg = [A.alloc("ystg%d" % i, [D]) for i in range(2)]

    def C(name):
        o, w = CONSTS[name]
        return cst[:, o:o + w]

    def CL(name, c=None):
        o, w = COLS[name]
        if c is None:
            return colt[:, o:o + w]
        return colt[:, o + c:o + c + 1]

    P.dma("sp", cst[:], const_d.ap, cst, writes=[cst])
    P.dma("sp", colt[:], cols_d.ap, colt, writes=[colt])
    m0 = A.mark()
    lstage = A.alloc("lstage", [1024])
    P.dma("sp", lstage[:], lora_d.ap, lstage, writes=[lstage])
    cp("act", lorab[:], lstage[:], [lstage], [lorab])
    A.release(m0)
    cp("dve", identb[:], C("ident"), [cst], [identb])
    cp("dve", blockb[:], C("blockm"), [cst], [blockb])
    mset("pool", ones64[:], 1.0, [ones64])
    o_mu, _ = COLS["mu"]
    ts("dve", omm[:], colt[:, o_mu:o_mu + 14], -1.0, 1.0, ALU.mult, ALU.add, [colt], [omm])
    o_ka, _ = COLS["ka"]
    ts("dve", omka[:], colt[:, o_ka:o_ka + 4], -1.0, 1.0, ALU.mult, ALU.add, [colt], [omka])
    ts("dve", nsink[:], C("sink"), -1.0, None, ALU.mult, None, [cst], [nsink])

    for b in range(NB):
        g = b // 4
        P.dma("pool", wbf[g].ap[b % 4], ws_d.ap[b], wbf[g], reads=[], writes=[wbf[g]])

    n_macro = NSEQ * nmt + (1 if HAS_S else 0)
    total_uses = n_macro * NB
    wstate = {"issued": 0, "used": 0}

    def w_issue():
        i = wstate["issued"]
        if i >= total_uses:
            return
        b = i % NB
        s = slots[i % NSLOT]
        P.dma("sp", s[:], wbf[b // 4].ap[b % 4], s, reads=[wbf[b // 4]], writes=[s])
        wstate["issued"] += 1

    def w_next(b):
        i = wstate["used"]
        assert i % NB == b, (i, b)
        while wstate["issued"] < min(total_uses, i + NSLOT - 1):
            w_issue()
        wstate["used"] += 1
        return slots[i % NSLOT]

    STOP = cfg.get("stop", None)
    import os as _os
    SKIP = _os.environ.get("KSKIP", "")
    OQ = cfg.get("outq", "sp")

    def chk(name):
        if STOP == name:
            raise StopBuild()

    def macro(kind, si, mi, xbuf):
        T_ = TP if kind == "p" else 128
        NS = T_ // 128
        nseg, L = (1, T_) if kind == "p" else (16, 8)
        first = (mi == 0)
        last = (kind == "s") or (mi == nmt - 1)
        xt = xts[xbuf]
        pre = "s" if kind == "s" else ""
        mk = A.mark()

        def norm_T(gname, hT):
            mkn = A.mark()
            ss = A.alloc("ss", [4])
            junk = A.alloc("junk", [D], BF16)
            for n in range(NS):
                act(junk[:], xt[:, n, :], AF.Square, [xt], [junk, ss], accum=ss[:, n:n + 1])
            ts("dve", ss[:, 0:NS], ss[:, 0:NS], 1.0 / D, NORM_EPS, ALU.mult, ALU.add, [ss], [ss])
            act(ss[:, 0:NS], ss[:, 0:NS], AF.Sqrt, [ss], [ss])
            recip(ss[:, 0:NS], ss[:, 0:NS], [ss], [ss])
            o_g, _ = COLS[gname]
            for n in range(NS):
                xn = A.alloc("xn", [D], BF16)
                ts("dve", xn[:], xt[:, n, :], ss[:, n:n + 1], None, ALU.mult, None, [xt, ss], [xn])
                ps = P.psum()
                psb = ps[:].bitcast(BF16)
                for c in range(8):
                    tr(psb[:, c * 128:(c + 1) * 128], xn[:, c * 128:(c + 1) * 128], identb[:], [xn, identb], [ps])
                gb = bc(colt.h.tensor, colt[:, o_g:o_g + 8].offset, [colt[:].ap[0], [1, 8], [0, 128]])
                tt("dve", hT[:, :, n * 128:(n + 1) * 128], psb.rearrange("p (c t) -> p c t", t=128), gb, ALU.mult,
                   [ps, colt], [hT])
                P.pfree(ps)
            A.release(mkn)

        hT = A.alloc("hT", [8, T_], BF16)
        norm_T("g_attn", hT)
        dump(pre + "hT", hT, hT[:], [128, 8, T_])
        chk("norm")

        qT = A.alloc("qT", [8, T_], BF16)
        gates = A.alloc("gates", [16, T_], BF16)
        buf = A.alloc("buf", [14, nseg, L + 1])
        oT = A.alloc("oT", [8, T_], BF16)
        orw = A.alloc("orw", [4, T_], BF16)
        kvst = A.alloc("kvst", [256])

        blk = w_next(0)
        bv = blk[:].rearrange("p (k c) -> p k c", c=512)
        for h in range(8):
            ps = P.psum()
            for kc in range(8):
                mm(ps[0:64, 0:T_], bv[:, kc, h * 64:(h + 1) * 64], hT[:, kc, :], kc == 0, kc == 7, [blk, hT], [ps])
            cp("act", qT[0:64, h, :], ps[0:64, 0:T_], [ps], [qT])
            P.pfree(ps)
        dump(pre + "qT0", qT, qT[0:64], [64, 8, T_])
        chk("q")
        blk = w_next(1)
        bv = blk[:].rearrange("p (k c) -> p k c", c=512)
        for kvh in range(2):
            ps = P.psum()
            for kc in range(8):
                mm(ps[0:64, 0:T_], bv[:, kc, kvh * 64:(kvh + 1) * 64], hT[:, kc, :], kc == 0, kc == 7, [blk, hT], [ps])
            cp("act", kT[0:64, kvh, 128:128 + T_], ps[0:64, 0:T_], [ps], [kT])
            P.pfree(ps)
        dump(pre + "kT", kT, kT[0:64], [64, 2, 128 + TP])
        chk("kv1")
        if kind == "p":
            for n in range(NS):
                ps = P.psum()
                for kc in range(8):
                    mm(ps[:, 0:256], hT[:, kc, n * 128:(n + 1) * 128], bv[:, kc, 0:256], kc == 0, kc == 7, [blk, hT], [ps])
                cp("act", vall[:, 1 + n, :], ps[:, 128:256], [ps], [vall])
                if STOP == "kv2":
                    P.pfree(ps)
                    continue
                if last and n == NS - 1:
                    cp("act", kvst[:], ps[:, 0:256], [ps], [kvst])
                    dump(pre + "kvst", kvst, kvst[:], [128, 256])
                    if STOP == "kv3":
                        P.pfree(ps)
                        continue
                    P.dma(OQ, wkp_d.ap[si], kvst[:, 0:128], kvst, reads=[kvst], writes=[wkp_d], final=True)
                    P.dma(OQ, wvp_d.ap[si], kvst[:, 128:256], kvst, reads=[kvst], writes=[wvp_d], final=True)
                P.pfree(ps)
        else:
            vnb = A.alloc("vnb", [16, 128], BF16)
            mkv = A.mark()
            kvn = A.alloc("kvn", [16, 256])
            for s in range(16):
                ps = P.psum()
                for kc in range(8):
                    mm(ps[0:8, 0:256], hT[:, kc, s * 8:(s + 1) * 8], bv[:, kc, 0:256], kc == 0, kc == 7, [blk, hT], [ps])
                cp("act", kvn[0:8, s, :], ps[0:8, 0:256], [ps], [kvn])
                P.pfree(ps)
            cp("dve", vnb[0:8, :, :], kvn[0:8, :, 128:256], [kvn], [vnb])
            P.dma(OQ, wks_d.ap[:, 0:120, :], ck_d.ap[:, 8:128, :], wks_d, reads=[], writes=[wks_d], final=True)
            P.dma(OQ, wvs_d.ap[:, 0:120, :], cv_d.ap[:, 8:128, :], wvs_d, reads=[], writes=[wvs_d], final=True)
            P.dma(OQ, wks_d.ap[:, 120:128, :].rearrange("s t f -> t s f"), kvn[0:8, :, 0:128], kvn, reads=[kvn], writes=[wks_d], final=True)
            P.dma(OQ, wvs_d.ap[:, 120:128, :].rearrange("s t f -> t s f"), kvn[0:8, :, 128:256], kvn, reads=[kvn], writes=[wvs_d], final=True)
            A.release(mkv)
        dump(pre + "vall", vall, vall[:], [128, 1 + TP // 128, 128])
        chk("kv2")
        chk("kv3")
        chk("kv")
        for c in range(14):
            if c % 4 == 0:
                blk = w_next(2 + c // 4)
                bv = blk[:].rearrange("p (k c) -> p k c", c=512)
            ps = P.psum()
            for kc in range(8):
                mm(ps[:, 0:T_], bv[:, kc, (c % 4) * 128:(c % 4 + 1) * 128], hT[:, kc, :], kc == 0, kc == 7, [blk, hT], [ps])
            cp("act", buf[:, c, :, 1:L + 1], ps[:, 0:T_].rearrange("p (s l) -> p s l", l=L), [ps], [buf])
            P.pfree(ps)
        chk("rw")
        for c in range(16):
            if c % 4 == 0:
                blk = w_next(6 + c // 4)
                bv = blk[:].rearrange("p (k c) -> p k c", c=512)
            ps = P.psum()
            for kc in range(8):
                mm(ps[:, 0:T_], bv[:, kc, (c % 4) * 128:(c % 4 + 1) * 128], hT[:, kc, :], kc == 0, kc == 7, [blk, hT], [ps])
            act(gates[:, c, :], ps[:, 0:T_], AF.Sigmoid, [ps, colt], [gates], bias=CL("bgate", c))
            P.pfree(ps)
        dump(pre + "qT", qT, qT[0:64], [64, 8, T_])
        dump(pre + "gates", gates, gates[:], [128, 16, T_])
        chk("win")

        if kind == "p":
            if first:
                mset("pool", buf[:, :, 0, 0:1], 0.0, [buf])
            else:
                cp("pool", buf[:, :, 0, 0:1], shc[:].rearrange("p (c o) -> p c o", o=1), [shc], [buf])
        else:
            mks = A.mark()
            sst = A.alloc("sst", [RIN])
            P.dma("pool", sst[0:16, :], ssh_d.ap, sst, writes=[sst])
            for g in range(4):
                ps = P.psum()
                ncg = 4 if g < 3 else 2
                for j in range(ncg):
                    c = g * 4 + j
                    tr(ps[:, j * 16:(j + 1) * 16], sst[0:16, c * 128:(c + 1) * 128], C("ident")[0:16, 0:16], [sst, cst], [ps])
                cp("dve", buf[:, g * 4:g * 4 + ncg, :, 0], ps[:, 0:ncg * 16].rearrange("p (c s) -> p c s", s=16), [ps], [buf])
                P.pfree(ps)
            A.release(mks)
        dump(pre + "buf", buf, buf[:], [128, 14, nseg, L + 1])

        if last:
            mksh = A.mark()
            shst = A.alloc("shst", [RIN])
            for g in range(4):
                ps = P.psum()
                ncg = 4 if g < 3 else 2
                for j in range(ncg):
                    c = g * 4 + j
                    tr(ps[0:nseg, j * 128:(j + 1) * 128], buf[:, c, :, L], C("ident"), [buf, cst], [ps])
                cp("act", shst[0:nseg, g * 512:g * 512 + ncg * 128], ps[0:nseg, 0:ncg * 128], [ps], [shst])
                P.pfree(ps)
            if kind == "p":
                P.dma(OQ, shp_d.ap[si:si + 1, :], shst[0:1, :], shst, reads=[shst], writes=[shp_d], final=True)
            else:
                P.dma(OQ, shs_d.ap, shst[0:16, :], shst, reads=[shst], writes=[shs_d], final=True)
            A.release(mksh)
        elif kind == "p":
            cp("pool", shc[:].rearrange("p (c o) -> p c o", o=1), buf[:, :, 0, L:L + 1], [buf], [shc])

        AR = A.alloc("AR", [4, NS, 2, 128], BF16)
        Bt = A.alloc("Bt", [4, T_], BF16)
        Kt = A.alloc("Kt", [4, T_], BF16)
        bhtok = A.alloc("bhtok", [NS, 512], BF16)
        khtok = A.alloc("khtok", [NS, 512], BF16)
        vtok = A.alloc("vtok", [NS, 512], BF16)
        gT = A.alloc("gT", [4, T_])
        bon = A.alloc("bon", [4, T_])
        yo = A.alloc("yo", [4, T_])
        E1 = A.alloc("E1", [4, T_])
        lin = A.alloc("lin", [2, T_], BF16)
        mk2 = A.mark()
        xs12 = A.alloc("xs12", [2, nseg, L])
        tA = A.alloc("tA", [2, nseg, L])

        def shiftmix(dst, j, c, tmp):
            act(tmp[:, j], buf[:, c, :, 0:L], AF.Identity, [buf, colt], [tmp], scale=CL("mu", c))
            stt("dve", dst[:, j], buf[:, c, :, 1:L + 1], omm[:, c:c + 1], tmp[:, j], ALU.mult, ALU.add, [buf, omm, tmp], [dst])

        shiftmix(xs12, 0, 12, tA)
        shiftmix(xs12, 1, 13, tA)
        x12 = xs12[:].rearrange("p c s l -> p c (s l)")
        act(lin[0:64, 0, :], x12[0:64, 0, :], AF.Tanh, [xs12], [lin])
        cp("act", lin[64:128, 0, :], x12[64:128, 0, :], [xs12], [lin])
        act(lin[:, 1, :], x12[:, 1, :], AF.Sigmoid, [xs12], [lin])
        A.release(mk2)

        def attn(M, qap, kprev, kcur, ncur, vprev, vcur, maskap, sinkap, nsinkap, out_ap, in_view, Rq, Rk, Rv, Rs, setidx):
            Wd = 128 + ncur
            s_, p_, pn, pT, st = atmp[setidx]
            ps = P.psum()
            mm(ps[0:M, 0:128], qap, kprev, True, True, Rq + Rk, [ps])
            mm(ps[0:M, 128:Wd], qap, kcur, True, True, Rq + Rk, [ps])
            stt("dve", s_[0:M, 0:Wd], ps[0:M, 0:Wd], 0.125, maskap, ALU.mult, ALU.add, [ps, cst], [s_])
            red("dve", st[0:M, 0:1], s_[0:M, 0:Wd], ALU.max, [s_], [st])
            stt("dve", st[0:M, 1:2], st[0:M, 0:1], -1.0, nsinkap, ALU.mult, ALU.min, [st] + Rs, [st])
            act(p_[0:M, 0:Wd], s_[0:M, 0:Wd], AF.Exp, [s_, st], [p_, st], bias=st[0:M, 1:2], accum=st[0:M, 2:3])
            act(st[0:M, 3:4], sinkap, AF.Exp, [st] + Rs, [st], bias=st[0:M, 1:2])
            tt("dve", st[0:M, 2:3], st[0:M, 2:3], st[0:M, 3:4], ALU.add, [st], [st])
            recip(st[0:M, 2:3], st[0:M, 2:3], [st], [st])
            ts("dve", pn[0:M, 0:Wd], p_[0:M, 0:Wd], st[0:M, 2:3], None, ALU.mult, None, [p_, st], [pn])
            psb = ps[:].bitcast(BF16)
            tr(psb[:, 512:512 + M], pn[0:M, 0:128], identb[0:M, 0:M], [pn, identb], [ps])
            tr(psb[0:ncur, 512 + M:512 + 2 * M], pn[0:M, 128:Wd], identb[0:M, 0:M], [pn, identb], [ps])
            cp("act", pT[:, 0, 0:M], psb[:, 512:512 + M], [ps], [pT])
            cp("act", pT[0:ncur, 1, 0:M], psb[0:ncur, 512 + M:512 + 2 * M], [ps], [pT])
            mm(ps[0:64, 384:384 + M], vprev, pT[:, 0, 0:M], True, False, Rv + [pT], [ps])
            mm(ps[0:64, 384:384 + M], vcur, pT[0:ncur, 1, 0:M], False, True, Rv + [pT], [ps])
            cp("act", out_ap, in_view(ps[0:64, 384:384 + M]), [ps], [oT])
            P.pfree(ps)

        acalls = []
        atmp = [(A.alloc("as%d" % i, [256]), A.alloc("ap%d" % i, [256]), A.alloc("apn%d" % i, [256], BF16),
                 A.alloc("apT%d" % i, [2, 128], BF16), A.alloc("ast%d" % i, [4])) for i in range(4)]

        if kind == "p":
            if first:
                mset("pool", kT[0:64, :, 0:128], 0.0, [kT])
                mset("pool", vall[:, 0, :], 0.0, [vall])
            for n in range(NS):
                mname = "maskA0" if (first and n == 0) else "maskA"
                for h in range(8):
                    kvh = h // 4
                    acalls.append(lambda si_, n=n, h=h, kvh=kvh, mname=mname: attn(
                        128, qT[0:64, h, n * 128:(n + 1) * 128], kT[0:64, kvh, n * 128:n * 128 + 128],
                        kT[0:64, kvh, n * 128 + 128:n * 128 + 256], 128,
                        vall[:, n, kvh * 64:(kvh + 1) * 64], vall[:, n + 1, kvh * 64:(kvh + 1) * 64],
                        C(mname), C("sink")[:, h:h + 1], nsink[:, h:h + 1],
                        oT[0:64, h, n * 128:(n + 1) * 128], lambda a: a, [qT], [kT], [vall], [cst, nsink], si_))
        else:
            vcb = A.alloc("vcb", [16, 128], BF16)
            kcT = A.alloc("kcT", [16, 2, 128], BF16)
            sinkS = A.alloc("sinkS", [4])
            mkk = A.mark()
            kcb = A.alloc("kcb", [16, 128], BF16)
            for q4 in range(4):
                mkc = A.mark()
                stg = A.alloc("cstg", [4, 128])
                P.dma("pool", stg[:], ck_d.ap[q4 * 4:(q4 + 1) * 4].rearrange("s w f -> w s f"), stg, writes=[stg])
                cp("dve", kcb[:, q4 * 4:(q4 + 1) * 4, :], stg[:], [stg], [kcb])
                stg2 = A.alloc("cstg2", [4, 128])
                P.dma("pool", stg2[:], cv_d.ap[q4 * 4:(q4 + 1) * 4].rearrange("s w f -> w s f"), stg2, writes=[stg2])
                cp("dve", vcb[:, q4 * 4:(q4 + 1) * 4, :], stg2[:], [stg2], [vcb])
                A.release(mkc)
            for s in range(16):
                ps = P.psum()
                psb = ps[:].bitcast(BF16)
                tr(psb[:, 0:128], kcb[:, s, :], identb[:], [kcb, identb], [ps])
                cp("act", kcT[0:64, s, 0, :], psb[0:64, 0:128], [ps], [kcT])
                cp("act", kcT[0:64, s, 1, :], psb[64:128, 0:128], [ps], [kcT])
                P.pfree(ps)
            A.release(mkk)
            cp("pool", sinkS[:, 0:2], C("sinkS"), [cst], [sinkS])
            ts("pool", sinkS[:, 2:4], C("sinkS"), -1.0, None, ALU.mult, None, [cst], [sinkS])
            qS = A.alloc("qS", [16, 2, 32], BF16)
            for kvh in range(2):
                cp("pool", qS[0:64, :, kvh, :].rearrange("p s (g t) -> p s g t", t=8),
                   qT[0:64, kvh * 4:(kvh + 1) * 4, :].rearrange("p g (s t) -> p s g t", t=8), [qT], [qS])
            for s in range(16):
                for kvh in range(2):
                    acalls.append(lambda si_, s=s, kvh=kvh: attn(
                        32, qS[0:64, s, kvh, :], kcT[0:64, s, kvh, :],
                        kT[0:64, kvh, 128 + s * 8:128 + (s + 1) * 8], 8,
                        vcb[:, s, kvh * 64:(kvh + 1) * 64], vnb[0:8, s, kvh * 64:(kvh + 1) * 64],
                        C("maskS")[0:32, 0:136], sinkS[0:32, kvh:kvh + 1], sinkS[0:32, 2 + kvh:3 + kvh],
                        oT[0:64, kvh * 4:(kvh + 1) * 4, s * 8:(s + 1) * 8],
                        lambda a: a.rearrange("p (g t) -> p g t", t=8), [qS], [kT, kcT], [vcb, vnb], [sinkS], si_))

        segm = C("segP")[:, 0:T_] if kind == "p" else C("segS")
        nsg, Ls = (NS, 128) if kind == "p" else (16, 8)
        def prep_c(c):
            xr = A.alloc("xr", [3, nseg, L])
            tB = A.alloc("tB", [3, nseg, L])
            for j in range(3):
                shiftmix(xr, j, c + 4 * j, tB)
            xrf = xr[:].rearrange("p c s l -> p c (s l)")
            r_, k_, v_ = xrf[:, 0, :], xrf[:, 1, :], xrf[:, 2, :]
            lw = A.alloc("lw", [T_])
            asg = A.alloc("asg", [T_])
            kkn = A.alloc("kkn", [T_])
            kmod = A.alloc("kmod", [T_])
            t1 = A.alloc("t1", [T_])
            t2 = A.alloc("t2", [T_])
            lP = A.alloc("lP", [T_])
            E2 = A.alloc("E2", [T_])
            E3 = A.alloc("E3", [T_])
            Dd = A.alloc("Dd", [T_])
            sqb = A.alloc("sqb", [T_], BF16)
            bhT = A.alloc("bhT", [T_], BF16)
            khT = A.alloc("khT", [T_], BF16)
            vTb = A.alloc("vTb", [T_], BF16)
            ps = P.psum()
            mm(ps[:, 0:T_], lorab[0:64, c * 128:(c + 1) * 128], lin[0:64, 0, :], True, True, [lorab, lin], [ps])
            act(lw[:], ps[:, 0:T_], AF.Sigmoid, [ps, colt], [lw], bias=CL("w0", c))
            P.pfree(ps)
            ps = P.psum()
            mm(ps[:, 0:T_], lorab[64:128, c * 128:(c + 1) * 128], lin[64:128, 0, :], True, True, [lorab, lin], [ps])
            act(asg[:], ps[:, 0:T_], AF.Sigmoid, [ps, colt], [asg], bias=CL("a0", c))
            P.pfree(ps)
            ps = P.psum()
            mm(ps[:, 0:T_], lorab[:, 512 + c * 128:512 + (c + 1) * 128], lin[:, 1, :], True, True, [lorab, lin], [ps])
            cp("act", gT[:, c, :], ps[:, 0:T_], [ps], [gT])
            P.pfree(ps)
            ts("dve", kkn[:], k_, CL("kk", c), None, ALU.mult, None, [xr, colt], [kkn])
            act(sqb[:], kkn[:], AF.Square, [kkn], [sqb])
            ps = P.psum()
            mm(ps[:, 0:T_], blockb[:], sqb[:], True, True, [blockb, sqb], [ps])
            act(t1[:], ps[:, 0:T_], AF.Sqrt, [ps], [t1])
            P.pfree(ps)
            ts("dve", t1[:], t1[:], 1e-12, None, ALU.max, None, [t1], [t1])
            recip(t1[:], t1[:], [t1], [t1])
            tt("dve", kkn[:], kkn[:], t1[:], ALU.mult, [kkn, t1], [kkn])
            act(t2[:], asg[:], AF.Identity, [asg, colt, omka], [t2], bias=omka[:, c:c + 1], scale=CL("ka", c))
            tt("pool", kmod[:], k_, t2[:], ALU.mult, [xr, t2], [kmod])
            tt("pool", t2[:], r_, kmod[:], ALU.mult, [xr, kmod], [t2])
            ts("dve", sqb[:], t2[:], CL("rk", c), None, ALU.mult, None, [t2, colt], [sqb])
            ps = P.psum()
            mm(ps[:, 0:T_], blockb[:], sqb[:], True, True, [blockb, sqb], [ps])
            tt("dve", bon[:, c, :], ps[:, 0:T_], v_, ALU.mult, [ps, xr], [bon])
            P.pfree(ps)
            P.op("dve", lambda e, o=lP[:], d0=segm, d1=lw[:]: e.tensor_tensor_scan(out=o, data0=d0, data1=d1, initial=0.0,
                                                                                  op0=ALU.mult, op1=ALU.add),
                 [cst, lw], [lP])
            act(E1[:, c, :], lP[:], AF.Exp, [lP], [E1], scale=-0.6065306597126334)
            act(E2[:], lP[:], AF.Exp, [lP], [E2], scale=0.6065306597126334)
            tt("pool", t1[:], lP[:], lw[:], ALU.subtract, [lP, lw], [t1])
            act(E3[:], t1[:], AF.Exp, [t1], [E3], scale=-0.6065306597126334)
            lP3 = lP[:].rearrange("p (g l) -> p g l", l=Ls)
            lastc = bc(lP.h.tensor, lP3[:, :, Ls - 1:Ls].offset, [lP[:].ap[0], [Ls, nsg], [0, Ls]])
            tt("dve", t2[:].rearrange("p (g l) -> p g l", l=Ls), lastc, lP3, ALU.subtract, [lP], [t2])
            act(Dd[:], t2[:], AF.Exp, [t2], [Dd], scale=-0.6065306597126334)
            stt("dve", AR[:, c, :, 0, :], kkn[:].rearrange("p (n t) -> p n t", t=128), -1.0,
                E3[:].rearrange("p (n t) -> p n t", t=128), ALU.mult, ALU.mult, [kkn, E3], [AR])
            tt("pool", AR[:, c, :, 1, :], r_.rearrange("p (n t) -> p n t", t=128),
               E1[:, c, :].rearrange("p (n t) -> p n t", t=128), ALU.mult, [xr, E1], [AR])
            tt("pool", t1[:], kkn[:], asg[:], ALU.mult, [kkn, asg], [t1])
            tt("pool", Bt[:, c, :], t1[:], E2[:], ALU.mult, [t1, E2], [Bt])
            tt("dve", Kt[:, c, :], kmod[:], E2[:], ALU.mult, [kmod, E2], [Kt])
            tt("pool", bhT[:], t1[:], Dd[:], ALU.mult, [t1, Dd], [bhT])
            tt("dve", khT[:], kmod[:], Dd[:], ALU.mult, [kmod, Dd], [khT])
            cp("act", vTb[:], v_, [xr], [vTb])
            for n in range(NS):
                ps = P.psum()
                psb = ps[:].bitcast(BF16)
                tr(psb[:, 0:128], bhT[:, n * 128:(n + 1) * 128], identb[:], [bhT, identb], [ps])
                tr(psb[:, 128:256], khT[:, n * 128:(n + 1) * 128], identb[:], [khT, identb], [ps])
                tr(psb[:, 256:384], vTb[:, n * 128:(n + 1) * 128], identb[:], [vTb, identb], [ps])
                cp("act", bhtok[:, n, c * 128:(c + 1) * 128], psb[:, 0:128], [ps], [bhtok])
                cp("act", khtok[:, n, c * 128:(c + 1) * 128], psb[:, 128:256], [ps], [khtok])
                cp("dve", vtok[:, n, c * 128:(c + 1) * 128], psb[:, 256:384], [ps], [vtok])
                P.pfree(ps)
            if c == 0:
                dump(pre + "lw0", lw, lw[:], [128, T_])
                dump(pre + "asg0", asg, asg[:], [128, T_])
                dump(pre + "kkn0", kkn, kkn[:], [128, T_])
                dump(pre + "kmod0", kmod, kmod[:], [128, T_])
                dump(pre + "lP0", lP, lP[:], [128, T_])

        half = len(acalls) // 2
        for pi, c0_ in enumerate((0, 2)):
            mk3 = A.mark()
            tasks = []
            assert len(P.psum_free) == 8
            allb = list(P.psum_all)
            P.psum_free = allb[0:4]
            for c in (c0_, c0_ + 1):
                P.begin_task()
                prep_c(c)
                tasks.append(P.end_task())
            assert len(P.psum_free) == 4
            P.psum_free = allb[4:8]
            calls = acalls[pi * half:(pi + 1) * half]
            per = len(calls) // 4
            for j in range(4):
                P.begin_task()
                for k in range(per):
                    want = allb[4 + (j + k) % 4]
                    P.psum_free.remove(want)
                    P.psum_free.insert(0, want)
                    calls[j * per + k](j)
                tasks.append(P.end_task())
            assert len(P.psum_free) == 4
            P.psum_free = allb
            P.interleave(tasks)
            A.release(mk3)
        if kind == "p" and not last:
            cp("pool", kT[0:64, :, 0:128], kT[0:64, :, T_:T_ + 128], [kT], [kT])
            cp("pool", vall[:, 0, :], vall[:, NS, :], [vall], [vall])
        dump(pre + "oT", oT, oT[0:64], [64, 8, T_])
        chk("attn")
        dump(pre + "bon", bon, bon[:], [128, 4, T_])
        dump(pre + "gT", gT, gT[:], [128, 4, T_])
        chk("prep")

        if kind == "p" and first:
            mset("pool", Sf[:], 0.0, [Sf])
            mset("pool", Sb[:], 0.0, [Sb])
        mp = "" if kind == "p" else "s"
        mkscan = A.mark()
        mUs, mLs = C(mp + "mUs"), C(mp + "mLs")
        o3, _ = CONSTS[mp + "mUi"]
        mask3 = cst[:, o3:o3 + 384]
        PQ = [[A.alloc("PQ%d_%d" % (n, c), [2, 2, 128], BF16) for c in range(4)] for n in range(NS)]
        ACC = [[A.alloc("ACC%d_%d" % (n, c), [2, 2, 128], BF16) for c in range(4)] for n in range(NS)]
        M3 = [[A.alloc("M3_%d_%d" % (n, h), [384], BF16) for h in range(8)] for n in range(NS)]
        Xs = A.alloc("Xs", [512], BF16)
        Us = A.alloc("Us", [512], BF16)
        tmpS = A.alloc("tmpS", [4, 128])
        ysb = A.alloc("ysb", [512])
        ydd = A.alloc("ydd", [512])
        yst = A.alloc("yst", [16])
        if kind == "s":
            X0 = A.alloc("X0", [512])
            Y0 = A.alloc("Y0", [512])
            E1tok = A.alloc("E1tok", [512])

        ib = bc(identb.h.tensor, identb[:].offset, [identb[:].ap[0], [0, 2], [1, 128]])

        def inv_task(n, c):
            PQc, ACCc = PQ[n][c], ACC[n][c]
            for h2 in range(2):
                h = 2 * c + h2
                b0 = h2 * 64
                psG = P.psum()
                arv = AR[b0:b0 + 64, c, n, :, :].rearrange("p a t -> p (a t)")
                mm(psG[:, 0:256], Bt[b0:b0 + 64, c, n * 128:(n + 1) * 128], arv, True, True, [Bt, AR], [psG])
                mm(psG[:, 256:512], Kt[b0:b0 + 64, c, n * 128:(n + 1) * 128], arv, True, True, [Kt, AR], [psG])
                tt("dve", PQc[:, h2, 0, :], psG[:, 0:128], mUs, ALU.mult, [psG, cst], [PQc])
                tt("dve", M3[n][h][:], psG[:, 128:512], mask3, ALU.mult, [psG, cst], [M3[n][h]])
                P.pfree(psG)
                psL = P.psum()
                mm(psL[:, 0:128], AR[b0:b0 + 64, c, n, 0, :], Bt[b0:b0 + 64, c, n * 128:(n + 1) * 128],
                   True, True, [AR, Bt], [psL])
                tt("dve", PQc[:, h2, 1, :], psL[:, 0:128], mLs, ALU.mult, [psL, cst], [PQc])
                P.pfree(psL)
            tt("dve", ACCc[:, :, 0, :], PQc[:, :, 0, :], ib, ALU.add, [PQc, identb], [ACCc])
            tt("dve", ACCc[:, :, 1, :], PQc[:, :, 1, :], ib, ALU.add, [PQc, identb], [ACCc])
            for lvl in range(1, 7):
                lastl = (lvl == 6)
                ps = P.psum()
                for h2 in range(2):
                    mm(ps[:, h2 * 256:h2 * 256 + 128], PQc[:, h2, 1, :], PQc[:, h2, 0, :], True, True, [PQc], [ps])
                    if not lastl:
                        mm(ps[:, h2 * 256 + 128:h2 * 256 + 256], PQc[:, h2, 0, :], PQc[:, h2, 1, :], True, True, [PQc], [ps])
                if lastl:
                    cp("act", PQc[:, :, 0, :], ps[:].rearrange("p (h a t) -> p h a t", a=2, t=128)[:, :, 0, :], [ps], [PQc])
                else:
                    cp("act", PQc[:].rearrange("p h a t -> p (h a t)"), ps[:], [ps], [PQc])
                P.pfree(ps)
                ps = P.psum()
                for h2 in range(2):
                    mm(ps[:, h2 * 256:h2 * 256 + 128], ACCc[:, h2, 1, :], PQc[:, h2, 0, :], True, True, [ACCc, PQc], [ps])
                    if not lastl:
                        mm(ps[:, h2 * 256 + 128:h2 * 256 + 256], PQc[:, h2, 0, :], ACCc[:, h2, 1, :], True, True, [ACCc, PQc], [ps])
                if lastl:
                    tt("dve", ACCc[:, :, 0, :], ps[:].rearrange("p (h a t) -> p h a t", a=2, t=128)[:, :, 0, :],
                       ACCc[:, :, 0, :], ALU.add, [ps, ACCc], [ACCc])
                else:
                    tt("dve", ACCc[:].rearrange("p h a t -> p (h a t)"), ps[:], ACCc[:].rearrange("p h a t -> p (h a t)"),
                       ALU.add, [ps, ACCc], [ACCc])
                P.pfree(ps)

        assert len(P.psum_free) == 8
        tasks = []
        for n in range(NS):
            for c in range(4):
                P.psum_free.append(P.psum_free.pop(0))
                P.begin_task()
                inv_task(n, c)
                tasks.append(P.end_task())
        P.interleave(tasks)
        chk("sc1")

        for n in range(NS):
            chk("sc2")
            if kind == "s":
                sample_s0_terms(AR, X0, Y0, E1, E1tok)
            psX = P.psum()
            for c in range(4):
                if kind == "p":
                    mm(psX[:, c * 128:(c + 1) * 128], AR[:, c, n, 0, :], Sb[:, c, :], True, False, [AR, Sb], [psX])
                for h2 in range(2):
                    h = 2 * c + h2
                    mm(psX[:, h * 64:(h + 1) * 64], M3[n][h][:, 128:256], vtok[:, n, h * 64:(h + 1) * 64], kind == "s", True,
                       [M3[n][h], vtok], [psX])
            if kind == "p":
                cp("act", Xs[:], psX[:], [psX], [Xs])
            else:
                tt("dve", Xs[:], psX[:], X0[:], ALU.add, [psX, X0], [Xs])
            P.pfree(psX)
            psU = P.psum()
            for h in range(8):
                mm(psU[:, h * 64:(h + 1) * 64], ACC[n][h // 2][:, h % 2, 0, :], Xs[:, h * 64:(h + 1) * 64], True, True,
                   [ACC[n][h // 2], Xs], [psU])
            cp("act", Us[:], psU[:], [psU], [Us])
            P.pfree(psU)
            if n == 0:
                dump(pre + "Us", Us, Us[:], [128, 512])
            chk("sc3")
            psY = P.psum()
            for c in range(4):
                if kind == "p":
                    mm(psY[:, c * 128:(c + 1) * 128], AR[:, c, n, 1, :], Sb[:, c, :], True, False, [AR, Sb], [psY])
                for h2 in range(2):
                    h = 2 * c + h2
                    mm(psY[:, h * 64:(h + 1) * 64], M3[n][h][:, 0:128], Us[:, h * 64:(h + 1) * 64], kind == "s", False, [M3[n][h], Us], [psY])
                    mm(psY[:, h * 64:(h + 1) * 64], M3[n][h][:, 256:384], vtok[:, n, h * 64:(h + 1) * 64], False, True, [M3[n][h], vtok], [psY])
            if kind == "p":
                psS = P.psum()
                for c in range(4):
                    mm(psS[:, c * 128:(c + 1) * 128], bhtok[:, n, c * 128:(c + 1) * 128], Us[:, c * 128:(c + 1) * 128], True, False,
                       [bhtok, Us], [psS])
                    mm(psS[:, c * 128:(c + 1) * 128], khtok[:, n, c * 128:(c + 1) * 128], vtok[:, n, c * 128:(c + 1) * 128], False, True,
                       [khtok, vtok], [psS])
                bmb = bc(cst.h.tensor, C("blockm").offset, [cst[:].ap[0], [0, 4], [1, 128]])
                tt("dve", tmpS[:], psS[:].rearrange("p (c t) -> p c t", t=128), bmb, ALU.mult, [psS, cst], [tmpS])
                P.pfree(psS)
                pcb = bc(E1.h.tensor, E1[:, 0, n * 128 + 127:n * 128 + 128].offset, [E1[:].ap[0], [T_, 4], [0, 128]])
                tt("dve", Sf[:], Sf[:], pcb, ALU.mult, [Sf, E1], [Sf])
                tt("dve", Sf[:], Sf[:], tmpS[:], ALU.add, [Sf, tmpS], [Sf])
                cp("act", Sb[:], Sf[:], [Sf], [Sb])
            else:
                sample_state_out(Us, vtok, bhtok, khtok, E1tok)
            chk("sc4")
            if kind == "s":
                tt("dve", ysb[:], psY[:], Y0[:], ALU.add, [psY, Y0], [ysb])
                ysrc, ysrcT = ysb[:], ysb
            else:
                cp("act", ysb[:], psY[:], [psY], [ysb])
                ysrc, ysrcT = ysb[:], ysb
            P.pfree(psY)
            if n == 0:
                dump(pre + "ysb", ysb, ysb[:], [128, 512])
            y3 = ysrc.rearrange("p (h i) -> p h i", i=64)
            red("dve", yst[:, 0:8], y3, ALU.add, [ysrcT], [yst])
            ts("dve", yst[:, 0:8], yst[:, 0:8], -1.0 / 64, None, ALU.mult, None, [yst], [yst])
            nmb = bc(yst.h.tensor, yst[:, 0:8].offset, [yst[:].ap[0], [1, 8], [0, 64]])
            d3 = ydd[:].rearrange("p (h i) -> p h i", i=64)
            tt("dve", d3, y3, nmb, ALU.add, [ysrcT, yst], [ydd])
            tt("pool", ysb[:], ydd[:], ydd[:], ALU.mult, [ydd], [ysb])
            red("dve", yst[:, 8:16], ysb[:].rearrange("p (h i) -> p h i", i=64), ALU.add, [ysb], [yst])
            ts("dve", yst[:, 8:16], yst[:, 8:16], 1.0 / 64, LNX_EPS, ALU.mult, ALU.add, [yst], [yst])
            act(yst[:, 8:16], yst[:, 8:16], AF.Sqrt, [yst], [yst])
            recip(yst[:, 8:16], yst[:, 8:16], [yst], [yst])
            rsb = bc(yst.h.tensor, yst[:, 8:16].offset, [yst[:].ap[0], [1, 8], [0, 64]])
            tt("dve", d3, d3, rsb, ALU.mult, [ydd, yst], [ydd])
            ps = P.psum()
            for c in range(4):
                tr(ps[:, c * 128:(c + 1) * 128], ydd[:, c * 128:(c + 1) * 128], C("ident"), [ydd, cst], [ps])
            for c in range(4):
                act(yo[:, c, n * 128:(n + 1) * 128], ps[:, c * 128:(c + 1) * 128], AF.Identity, [ps, colt], [yo],
                    bias=CL("lnb", c), scale=CL("lnw", c))
            P.pfree(ps)
        tt("dve", yo[:], yo[:], bon[:], ALU.add, [yo, bon], [yo])
        tt("dve", orw[:], yo[:], gT[:], ALU.mult, [yo, gT], [orw])
        dump(pre + "orw", orw, orw[:], [128, 4, T_])
        chk("scan")
        if kind == "p" and last:
            sst2 = A.alloc("sst2", [8, 64])
            ps = P.psum()
            for c in range(4):
                tr(ps[:, c * 128:(c + 1) * 128], Sf[:, c, :], C("ident"), [Sf, cst], [ps])
            for c in range(4):
                for h2 in range(2):
                    cp("act", sst2[0:64, 2 * c + h2, :], ps[h2 * 64:(h2 + 1) * 64, c * 128 + h2 * 64:c * 128 + h2 * 64 + 64],
                       [ps], [sst2])
            P.pfree(ps)
            P.dma(OQ, wkvp_d.ap[si].rearrange("h i j -> i h j"), sst2[0:64, :, :], sst2, reads=[sst2], writes=[wkvp_d], final=True)

        A.release(mkscan)
        merged = A.alloc("merged", [8, T_], BF16)
        mt1 = A.alloc("mt1", [T_])
        mt2 = A.alloc("mt2", [T_])
        baT = A.alloc("baT", [8, T_])
        for hf in range(2):
            wblk = w_next(10 + hf)
            wv = wblk[:].rearrange("p (h c) -> p h c", c=512)
            for cc in range(4):
                c = hf * 4 + cc
                ps = P.psum()
                for h in range(8):
                    mm(ps[:, 0:T_], wv[0:64, h, cc * 128:(cc + 1) * 128], oT[0:64, h, :], h == 0, h == 7, [wblk, oT], [ps])
                tt("dve", baT[:, c, :], ps[:, 0:T_], gates[:, c, :], ALU.mult, [ps, gates], [baT])
                P.pfree(ps)
        wblk = w_next(12)
        wv = wblk[:].rearrange("p (k c) -> p k c", c=1024)
        for c in range(8):
            ps = P.psum()
            for k4 in range(4):
                mm(ps[:, 0:T_], wv[:, k4, c * 128:(c + 1) * 128], orw[:, k4, :], k4 == 0, k4 == 3, [wblk, orw], [ps])
            tt("dve", mt1[:], ps[:, 0:T_], gates[:, 8 + c, :], ALU.mult, [ps, gates], [mt1])
            P.pfree(ps)
            tt("pool", merged[:, c, :], mt1[:], baT[:, c, :], ALU.add, [mt1, baT], [merged])
        dump(pre + "merged", merged, merged[:], [128, 8, T_])
        for hf in range(2):
            wblk = w_next(13 + hf)
            wv = wblk[:].rearrange("p (k c) -> p k c", c=512)
            for n in range(NS):
                ps = P.psum()
                for kc in range(8):
                    mm(ps[:], merged[:, kc, n * 128:(n + 1) * 128], wv[:, kc, :], kc == 0, kc == 7, [wblk, merged], [ps])
                tt("dve", xt[:, n, hf * 512:(hf + 1) * 512], xt[:, n, hf * 512:(hf + 1) * 512], ps[:], ALU.add, [xt, ps], [xt])
                P.pfree(ps)
        dump(pre + "x1", xt, xt[:, 0:NS, :], [128, NS, D])
        chk("merge")
        A.release(mk)

        mk = A.mark()
        hT = A.alloc("h2T", [8, T_], BF16)
        norm_T("g_ffn", hT)
        actT = A.alloc("actT", [NFC, T_], BF16)
        if kind == "p":
            if first:
                mset("pool", cvc[:], 0.0, [cvc])
            carry = cvc
        else:
            carry = None
            cst_s = A.alloc("cst_s", [DFF])
            P.dma("pool", cst_s[0:32, :], scv_d.ap, cst_s, writes=[cst_s])
            cvs = A.alloc("cvs", [NFC, 16, 2])
            for g in range(6):
                ncg = 4 if g < 5 else 2
                ps = P.psum()
                for j in range(ncg):
                    cc = g * 4 + j
                    tr(ps[:, j * 32:(j + 1) * 32], cst_s[0:32, cc * 128:(cc + 1) * 128], C("ident")[0:32, 0:32], [cst_s, cst], [ps])
                cp("dve", cvs[:, g * 4:g * 4 + ncg].rearrange("p c s j -> p c (s j)"), ps[:, 0:ncg * 32].rearrange("p (c x) -> p c x", x=32),
                   [ps], [cvs])
                P.pfree(ps)
            cvo = A.alloc("cvo", [NFC, 16, 2])
        NFS = 3
        gbs = [A.alloc("gbuf%d" % i, [nseg, L + 2]) for i in range(NFS)]
        fcs = [[A.alloc("fc%d_%d" % (i, k), [nseg, L]) for k in range(3)] for i in range(NFS)]
        def ffn_cc(cc, wblk):
            wv = wblk[:].rearrange("p (k c) -> p k c", c=512)
            j = cc % 2
            psg = P.psum()
            psv = P.psum()
            for kc in range(8):
                mm(psg[:, 0:T_], wv[:, kc, j * 128:(j + 1) * 128], hT[:, kc, :], kc == 0, kc == 7, [wblk, hT], [psg])
            for kc in range(8):
                mm(psv[:, 0:T_], wv[:, kc, 256 + j * 128:256 + (j + 1) * 128], hT[:, kc, :], kc == 0, kc == 7, [wblk, hT], [psv])
            g3 = psg[:, 0:T_].rearrange("p (s l) -> p s l", l=L)
            slot = cc % NFS
            gbT = gbs[slot]
            gq = gbT[:]
            f0, f1, f2 = fcs[slot]
            c0, c1, c2 = f0[:], f1[:], f2[:]
            if kind == "p":
                cp("pool", gq[:, 0, 0:2], cvc[:, cc, :], [cvc], [gbT])
            else:
                cp("pool", gq[:, :, 0:2], cvs[:, cc, :, :], [cvs], [gbT])
            cp("act", gq[:, :, 2:L + 2], g3, [psg], [gbT])
            act(c0, g3, AF.Identity, [psg, colt], [f0], bias=CL("cb", cc), scale=CL("cw2", cc))
            stt("dve", c1, gq[:, :, 1:L + 1], CL("cw1", cc), c0, ALU.mult, ALU.add, [gbT, colt, f0], [f1])
            stt("dve", c0, gq[:, :, 0:L], CL("cw0", cc), c1, ALU.mult, ALU.add, [gbT, colt, f1], [f0])
            if kind == "p":
                cp("pool", cvc[:, cc, :], gq[:, 0, L:L + 2], [gbT], [cvc])
            else:
                cp("pool", cvo[:, cc, :, :], gq[:, :, L:L + 2], [gbT], [cvo])
            act(c1, c0, AF.Square, [f0], [f1], scale=0.26712319)
            stt("dve", c1, c1, 1.5957691216057308, c0, ALU.add, ALU.mult, [f1, f0], [f1])
            act(c2, c1, AF.Sigmoid, [f1], [f2])
            tt("pool", c2, c2, c0, ALU.mult, [f2, f0], [f2])
            tt("dve", actT[:, cc, :].rearrange("p (s l) -> p s l", l=L), c2, psv[:, 0:T_].rearrange("p (s l) -> p s l", l=L), ALU.mult,
               [f2, psv], [actT])
            P.pfree(psg)
            P.pfree(psv)

        blkmap = {}
        FG = int(_os.environ.get("FFG", "3"))
        for g0 in range(0, NFC, FG):
            grp = list(range(g0, min(NFC, g0 + FG)))
            for cc in grp:
                if cc // 2 not in blkmap:
                    blkmap[cc // 2] = w_next(15 + cc // 2)
            tasks = []
            for cc in grp:
                P.begin_task()
                ffn_cc(cc, blkmap[cc // 2])
                tasks.append(P.end_task())
            P.interleave(tasks)
        dump(pre + "actT", actT, actT[:], [128, NFC, T_])
        chk("ffn")
        if last:
            cvst = A.alloc("cvst", [DFF])
            nr = 2 if kind == "p" else 32
            for g in range(6):
                ncg = 4 if g < 5 else 2
                ps = P.psum()
                for j in range(ncg):
                    cc = g * 4 + j
                    src = cvc[:, cc, :] if kind == "p" else cvo[:, cc].rearrange("p s j -> p (s j)")
                    tr(ps[0:nr, j * 128:(j + 1) * 128], src, C("ident"), [cvc if kind == "p" else cvo, cst], [ps])
                cp("act", cvst[0:nr, g * 512:g * 512 + ncg * 128], ps[0:nr, 0:ncg * 128], [ps], [cvst])
                P.pfree(ps)
            if kind == "p":
                P.dma(OQ, cvp_d.ap[si], cvst[0:2, :], cvst, reads=[cvst], writes=[cvp_d], final=True)
            else:
                P.dma(OQ, cvs_d.ap, cvst[0:32, :], cvst, reads=[cvst], writes=[cvs_d], final=True)
        for hf in range(2):
            pss = [P.psum() for _ in range(NS)]
            for gi in range(3):
                wblk = w_next(26 + hf * 3 + gi)
                wv = wblk[:].rearrange("p (k c) -> p k c", c=512)
                ncc = 8 if gi < 2 else 6
                for n in range(NS):
                    for k in range(ncc):
                        cc = gi * 8 + k
                        mm(pss[n][:], actT[:, cc, n * 128:(n + 1) * 128], wv[:, k, :], cc == 0, cc == NFC - 1, [wblk, actT], [pss[n]])
            for n in range(NS):
                tt("dve", xt[:, n, hf * 512:(hf + 1) * 512], xt[:, n, hf * 512:(hf + 1) * 512], pss[n][:], ALU.add, [xt, pss[n]], [xt])
                P.pfree(pss[n])
        ss2 = A.alloc("ss2", [4])
        junk2 = A.alloc("junk2", [D], BF16)
        for n in range(NS):
            act(junk2[:], xt[:, n, :], AF.Square, [xt], [junk2, ss2], accum=ss2[:, n:n + 1])
        ts("dve", ss2[:, 0:NS], ss2[:, 0:NS], 1.0 / D, NORM_EPS, ALU.mult, ALU.add, [ss2], [ss2])
        act(ss2[:, 0:NS], ss2[:, 0:NS], AF.Sqrt, [ss2], [ss2])
        recip(ss2[:, 0:NS], ss2[:, 0:NS], [ss2], [ss2])
        for n in range(NS):
            yb = ystg[n % 2]
            stt("dve", yb[:], xt[:, n, :], ss2[:, n:n + 1], C("gfin"), ALU.mult, ALU.mult, [xt, ss2, cst], [yb])
            if kind == "p":
                t0 = mi * TP + n * 128
                P.dma(OQ, yp_d.ap[si, t0:t0 + 128, :], yb[:], yb, reads=[yb], writes=[yp_d], final=True)
            else:
                P.dma(OQ, ys_d.ap, yb[:], yb, reads=[yb], writes=[ys_d], final=True)
        A.release(mk)

    sstate = {}

    def sample_s0_terms(AR, X0, Y0, E1, E1tok):
        ps = P.psum()
        for c in range(4):
            tr(ps[:, c * 128:(c + 1) * 128], E1[:, c, 0:128], C("ident"), [E1, cst], [ps])
        cp("act", E1tok[:], ps[:], [ps], [E1tok])
        P.pfree(ps)
        smb = bc(cst.h.tensor, C("seqm").offset, [cst[:].ap[0], [1, 16], [0, 64]])
        for h in range(8):
            c, h2 = h // 2, h % 2
            b0 = h2 * 64
            mk_h = A.mark()
            S0h = A.alloc("S0h", [16, 64])
            S0T = A.alloc("S0T", [16, 64], BF16)
            P.dma("pool", S0h[0:64, :, :], swkv_d.ap[:, h, :, :].rearrange("s i j -> i s j"), S0h, writes=[S0h])
            for q in range(2):
                ps = P.psum()
                for s8 in range(8):
                    s = q * 8 + s8
                    tr(ps[0:64, s8 * 64:(s8 + 1) * 64], S0h[0:64, s, :], C("ident")[0:64, 0:64], [S0h, cst], [ps])
                cp("act", S0T[b0:b0 + 64, q * 8:(q + 1) * 8, :], ps[0:64, :].rearrange("p (s i) -> p s i", i=64), [ps], [S0T])
                P.pfree(ps)
            for which, dst in ((0, X0), (1, Y0)):
                mk_ = A.mark()
                tmp = A.alloc("s0tmp", [16, 64])
                for q in range(2):
                    ps = P.psum()
                    mm(ps[:], AR[b0:b0 + 64, c, 0, which, :], S0T[b0:b0 + 64, q * 8:(q + 1) * 8, :].rearrange("p s i -> p (s i)"),
                       True, True, [AR, S0T], [ps])
                    smq = bc(cst.h.tensor, C("seqm")[:, q * 8:q * 8 + 8].offset, [cst[:].ap[0], [1, 8], [0, 64]])
                    tt("dve", tmp[:, q * 8:(q + 1) * 8, :], ps[:].rearrange("p (s i) -> p s i", i=64), smq, ALU.mult, [ps, cst], [tmp])
                    P.pfree(ps)
                red("dve", dst[:, h * 64:(h + 1) * 64], tmp[:].rearrange("p s i -> p i s"), ALU.add, [tmp], [dst])
                A.release(mk_)
            A.release(mk_h)

    def sample_state_out(Us, vtok, bhtok, khtok, E1tok):
        for h in range(8):
            mk_ = A.mark()
            S0h = A.alloc("S0h2", [16, 64])
            P.dma("pool", S0h[0:64, :, :], swkv_d.ap[:, h, :, :].rearrange("s i j -> i s j"), S0h, writes=[S0h])
            bhm = A.alloc("bhm", [16, 64], BF16)
            khm = A.alloc("khm", [16, 64], BF16)
            Wm = A.alloc("Wm", [16, 64])
            So = A.alloc("So", [16, 64])

            def b2(t, col0):
                return bc(t.h.tensor, t[:, 0, col0:col0 + 64].offset, [t[:].ap[0], [0, 16], [1, 64]])
            smb = bc(cst.h.tensor, C("seqm").offset, [cst[:].ap[0], [1, 16], [0, 64]])
            lmb = bc(cst.h.tensor, C("lastm").offset, [cst[:].ap[0], [1, 16], [0, 64]])
            tt("dve", bhm[:], b2(bhtok, h * 64), smb, ALU.mult, [bhtok, cst], [bhm])
            tt("dve", khm[:], b2(khtok, h * 64), smb, ALU.mult, [khtok, cst], [khm])
            e1b = bc(E1tok.h.tensor, E1tok[:, h * 64:(h + 1) * 64].offset, [E1tok[:].ap[0], [0, 16], [1, 64]])
            tt("dve", Wm[:], e1b, lmb, ALU.mult, [E1tok, cst], [Wm])
            for q in range(2):
                psA = P.psum()
                psP = P.psum()
                mm(psA[0:64, :], Us[:, h * 64:(h + 1) * 64], bhm[:, q * 8:(q + 1) * 8, :].rearrange("p s j -> p (s j)"), True, False, [Us, bhm], [psA])
                mm(psA[0:64, :], vtok[:, 0, h * 64:(h + 1) * 64], khm[:, q * 8:(q + 1) * 8, :].rearrange("p s j -> p (s j)"), False, True, [vtok, khm], [psA])
                mm(psP[0:64, :], ones64[:, 0:64], Wm[:, q * 8:(q + 1) * 8, :].rearrange("p s j -> p (s j)"), True, True, [ones64, Wm], [psP])
                so = So[0:64, q * 8:(q + 1) * 8, :].rearrange("p s j -> p (s j)")
                tt("dve", so, S0h[0:64, q * 8:(q + 1) * 8, :].rearrange("p s j -> p (s j)"), psP[0:64, :], ALU.mult, [S0h, psP], [So])
                tt("dve", so, so, psA[0:64, :], ALU.add, [So, psA], [So])
                P.pfree(psA)
                P.pfree(psP)
            P.dma(OQ, wkvs_d.ap[:, h, :, :].rearrange("s i j -> i s j"), So[0:64, :, :], So, reads=[So], writes=[wkvs_d], final=True)
            A.release(mk_)

    xi = 0
    sched = [("p", si, mi) for si in range(NSEQ) for mi in range(nmt)]
    if HAS_S:
        sched.append(("s", 0, 0))

    def load_x(k, xbuf):
        kind, si, mi = sched[k]
        xt = xts[xbuf]
        if kind == "p":
            for n in range(TP // 128):
                t0 = mi * TP + n * 128
                P.dma("sp", xt[:, n, :], xp_d.ap[si, t0:t0 + 128, :], xt, reads=[], writes=[xt])
        else:
            P.dma("sp", xt[:, 0, :], xs_d.ap, xt, reads=[], writes=[xt])

    load_x(0, 0)
    for k in range(len(sched)):
        if k + 1 < len(sched):
            load_x(k + 1, (k + 1) % 2)
        kind, si, mi = sched[k]
        try:
            macro(kind, si, mi, k % 2)
        except StopBuild:
            break

    P.emit(list(P.outdeps))
    stats = {e: P.n[e] for e in ENGS}
    stats["sems"] = P.nsem
    stats["arena_peak_bytes"] = A.peak * 4
    return nc, stats, list(dbg_out.keys())


def prep_shared(inp):
    l = 0
    ws = pack_weights(inp["w_in"][l], inp["w_branch_attn"][l], inp["w_branch_rwkv"][l], inp["w_out"][l],
                      inp["w_ffn_in"][l], inp["w_ffn_out"][l])
    cols = np.zeros((128, NCOLS), np.float32)

    def put(n, a):
        o, w = COLS[n]
        cols[:, o:o + w] = a
    put("g_attn", fm(inp["norm_attn"][l], 8)); put("g_ffn", fm(inp["norm_ffn"][l], 8))
    put("mu", fm(inp["mu_shift"][l], 14)); put("bgate", fm(inp["b_gate"][l], 16))
    put("w0", fm(inp["w0"][l], 4)); put("a0", fm(inp["a0"][l], 4)); put("kk", fm(inp["k_k"][l], 4))
    put("ka", fm(inp["k_a"][l], 4)); put("rk", fm(inp["r_k"][l].reshape(-1), 4))
    put("lnw", fm(inp["lnx_w"][l], 4)); put("lnb", fm(inp["lnx_b"][l], 4))
    put("cb", fm(inp["ffn_conv_b"][l], NFC))
    for j in range(3):
        put("cw%d" % j, fm(inp["ffn_conv_w"][l][j], NFC))
    consts = make_consts(np.asarray(inp["norm_final"]), np.asarray(inp["attn_sinks"][l]))
    lora = np.zeros((128, 1024), np.float32)
    lora[0:64, 0:512] = inp["w_lora_up"][l]
    lora[64:128, 0:512] = inp["a_lora_up"][l]
    lora[:, 512:1024] = inp["g_lora_up"][l]
    return {"wstream": ws, "cols": cols, "consts": consts, "lora": lora}


_CACHE = {}


def run(inp, n_cores, nseq, seq, tp, sample=True, dbg=None, stop=None):
    inp = {k: np.asarray(v) for k, v in inp.items()}
    cfg = dict(nseq=nseq, seq=seq, tp=tp, sample=sample, dbg=dbg, stop=stop)
    key = (nseq, seq, tp, sample, tuple(dbg) if dbg else None, stop)
    if key not in _CACHE:
        _CACHE[key] = build(cfg)
    nc, stats, dbgn = _CACHE[key]
    shared = prep_shared(inp)
    in_maps = []
    for c in range(n_cores):
        m = dict(shared)
        m["xp"] = np.ascontiguousarray(inp["x_prompt"][c * nseq:(c + 1) * nseq])
        if sample:
            sl = slice(c * 16, (c + 1) * 16)
            m["xsm"] = np.ascontiguousarray(inp["x_sample"][sl].reshape(128, D))
            m["ck"] = np.ascontiguousarray(inp["cache_win_k"][0, sl].reshape(16, 128, 128))
            m["cv"] = np.ascontiguousarray(inp["cache_win_v"][0, sl].reshape(16, 128, 128))
            m["ssh"] = np.ascontiguousarray(inp["state_shift"][0, sl])
            m["swkv"] = np.ascontiguousarray(inp["state_wkv"][0, sl])
            m["scv"] = np.ascontiguousarray(inp["state_ffn_conv"][0, sl].reshape(32, DFF))
        in_maps.append(m)
    res = run_bass_kernel_spmd(nc, in_maps, core_ids=list(range(n_cores)))
    R = res.results
    cat = lambda n: np.concatenate([r[n] for r in R], axis=0)
    B = n_cores * nseq
    out = [cat("yp"),
           None,
           cat("wkp").reshape(1, B, 128, 2, 64), cat("wvp").reshape(1, B, 128, 2, 64),
           cat("shp").reshape(1, B, RIN), cat("wkvp").reshape(1, B, 8, 64, 64), cat("cvp").reshape(1, B, 2, DFF)]
    if sample:
        S = n_cores * 16
        out[1] = cat("ys").reshape(S, 8, D)
        out += [cat("wks").reshape(1, S, 128, 2, 64), cat("wvs").reshape(1, S, 128, 2, 64),
                cat("shs").reshape(1, S, RIN), cat("wkvs").reshape(1, S, 8, 64, 64), cat("cvs").reshape(1, S, 2, DFF)]
    dbgv = {n: [r["dbg_" + n] for r in R] for n in dbgn}
    return tuple(out), dbgv, stats


def kernel(**inputs):
    out, _, _ = run(inputs, 8, 2, 2048, 256, True, None)
    return tuple(np.ascontiguousarray(o, dtype=np.float32) for o in out)
```

```python
import numpy as np
import concourse.bass as bass
import concourse.mybir as mybir
from concourse.ap import AP
from concourse.bass_utils import run_bass_kernel_spmd

F32 = mybir.dt.float32
BF16 = mybir.dt.bfloat16
ALU = mybir.AluOpType
AF = mybir.ActivationFunctionType
AX = mybir.AxisListType

ENGS = ("pe", "act", "dve", "pool", "sp")
EPOCH = 12000

D = 1024
HD = 64
NQH = 8
NKV = 2
RW = 512
RIN = 1792
DFF = 2816
NFC = 22
NB = 32
NORM_EPS = 1e-6
LNX_EPS = 64e-5
NEG = -30000.0


class T:
    def __init__(self, name, h=None):
        self.name = name
        self.h = h
        self.last_w = None
        self.readers = []
        self.dsem = None
        self.dcount = 0
        self.overlaps = []

    def __getitem__(self, k):
        return self.h[k]


class Prog:
    def __init__(self, nc):
        self.nc = nc
        self.q = {e: [] for e in ENGS}
        self.n = {e: 0 for e in ENGS}
        self.sems = {}
        self.water = {e: {} for e in ENGS}
        self.nsem = 0
        self.psum_free = []
        self.psum_all = []
        self.uid = 0
        self.outdeps = []
        self.rec = None
        self.rec_banks = []
        self.task_banks = {}

    def sem(self, key):
        if key not in self.sems:
            self.sems[key] = self.nc.alloc_semaphore("s%d" % self.nsem)
            self.nsem += 1
        return self.sems[key]

    def dram(self, name, shape, dt, kind="Internal"):
        t = T(name, None)
        t.d = self.nc.dram_tensor(name, list(shape), dt, kind=kind)
        t.ap = t.d.ap()
        t.is_out = (kind == "ExternalOutput")
        return t

    def psum_init(self):
        for i in range(8):
            t = T("ps%d" % i, self.nc.alloc_psum_tensor("ps%d" % i, [128, 512], F32))
            t.psum = True
            self.psum_all.append(t)
        self.psum_free = list(self.psum_all)

    def psum(self):
        assert self.psum_free, "out of PSUM banks"
        t = self.psum_free.pop(0)
        if self.rec is not None:
            self.rec_banks.append([t.name, len(self.rec), None])
        return t

    def pfree(self, t):
        assert t not in self.psum_free
        self.psum_free.append(t)
        if self.rec is not None:
            for e in self.rec_banks:
                if e[0] == t.name and e[2] is None:
                    e[2] = len(self.rec)

    def _deps(self, eng, reads, writes):
        deps = []
        for t in reads:
            if t.last_w is not None:
                deps.append(t.last_w)
            if getattr(t, "psum", False):
                deps.extend(r for r in t.readers if r[0][0] != eng)
        for t in writes:
            for tt in [t] + t.overlaps:
                if tt.last_w is not None:
                    deps.append(tt.last_w)
                deps.extend(tt.readers)
        best = {}
        for (k, v) in deps:
            if eng == "pe" and k[0] == "pe":
                continue
            if v > best.get(k, 0):
                best[k] = v
        out = []
        for k, v in best.items():
            if self.water[eng].get(k, 0) >= v:
                continue
            self.water[eng][k] = v
            out.append((k, v))
        return out

    def _commit(self, dep, reads, writes):
        for t in reads:
            t.readers.append(dep)
            if len(t.readers) > 48:
                best = {}
                for k, v in t.readers:
                    if v > best.get(k, 0):
                        best[k] = v
                t.readers = list(best.items())
        for t in writes:
            t.last_w = dep
            t.readers = []
            t.overlaps = []

    def begin_task(self):
        assert self.rec is None
        self.rec = []
        self.rec_banks = []

    def end_task(self):
        r, self.rec = self.rec, None
        self.task_banks[id(r)] = self.rec_banks
        return r

    def interleave(self, tasks):
        uses = []
        for k, t in enumerate(tasks):
            for (nm, i0, i1) in self.task_banks.get(id(t), []):
                uses.append((nm, i0, len(t) if i1 is None else i1, k))
        for x in uses:
            for y in uses:
                if x[0] == y[0] and x[3] < y[3]:
                    assert x[2] <= y[1] or y[2] <= x[1], ("psum bank overlap between interleaved tasks", x, y)
        idx = [0] * len(tasks)
        left = sum(len(t) for t in tasks)
        while left:
            for k, t in enumerate(tasks):
                if idx[k] < len(t):
                    kind, args = t[idx[k]]
                    idx[k] += 1
                    left -= 1
                    if kind == "op":
                        self.op(*args)
                    else:
                        self.dma(*args[:-1], final=args[-1])

    def op(self, eng, fn, reads=(), writes=()):
        if self.rec is not None:
            self.rec.append(("op", (eng, fn, list(reads), list(writes))))
            return None
        reads = [t for t in reads if t is not None]
        writes = [t for t in writes if t is not None]
        waits = self._deps(eng, reads, writes)
        idx = self.n[eng]
        self.n[eng] += 1
        key = (eng, idx // EPOCH)
        val = idx % EPOCH + 1
        self.sem(key)
        self.q[eng].append((waits, fn, key, 1))
        dep = (key, val)
        self._commit(dep, reads, writes)
        return dep

    def dma(self, eng, out_ap, in_ap, sbt, reads=(), writes=(), final=False):
        if self.rec is not None:
            self.rec.append(("dma", (eng, out_ap, in_ap, sbt, list(reads), list(writes), final)))
            return None
        reads = [t for t in reads if t is not None]
        writes = [t for t in writes if t is not None and not (final and getattr(t, "is_out", False))]
        waits = self._deps(eng, reads, writes)
        if sbt.dsem is None:
            self.uid += 1
            sbt.dsem = ("d", "%s_%d" % (sbt.name, self.uid))
            self.sem(sbt.dsem)
        sbt.dcount += 1
        key = sbt.dsem
        val = 16 * sbt.dcount

        def fn(e, out_ap=out_ap, in_ap=in_ap):
            return e.dma_start(out=out_ap, in_=in_ap)
        self.q[eng].append((waits, fn, key, 16))
        dep = (key, val)
        self._commit(dep, reads, writes)
        if final:
            self.outdeps.append(dep)
        return dep

    def emit(self, final_deps):
        nc = self.nc
        fin = {}
        for k, v in final_deps:
            if v > fin.get(k, 0):
                fin[k] = v
        P = self

        def run(eng_name):
            def body(e):
                for waits, fn, key, inc in P.q[eng_name]:
                    for k, v in waits:
                        e.wait_ge(P.sems[k], v)
                    ins = fn(e)
                    ins.then_inc(P.sems[key], inc)
                if eng_name == "sp":
                    for k, v in fin.items():
                        e.wait_ge(P.sems[k], v)
            return body

        with nc.Block() as block:
            block.sync(run("sp"))
            block.tensor(run("pe"))
            block.scalar(run("act"))
            block.vector(run("dve"))
            block.gpsimd(run("pool"))


class StopBuild(Exception):
    pass


class Arena:
    def __init__(self, P, words):
        self.P = P
        self.words = words
        self.h = P.nc.alloc_sbuf_tensor("arena", [128, words], F32)
        self.top = 0
        self.ghosts = []
        self.live = []
        self.peak = 0

    def alloc(self, name, shape, dt=F32):
        n = int(np.prod(shape))
        w = n if dt == F32 else (n + 1) // 2
        w = (w + 7) // 8 * 8
        s, e = self.top, self.top + w
        if e > self.words:
            print("ARENA OVERFLOW", name, e, [(t.name, (b - a) * 4) for a, b, t in self.live if (b - a) * 4 >= 2000])
        assert e <= self.words, "arena overflow %s %d" % (name, e)
        self.top = e
        self.peak = max(self.peak, e)
        ap = self.h[:, s:e]
        if dt != F32:
            ap = ap.bitcast(dt)
        ap = ap[:, 0:n]
        if len(shape) == 2:
            ap = ap.rearrange("p (a b) -> p a b", b=shape[1])
        elif len(shape) == 3:
            ap = ap.rearrange("p (a b c) -> p a b c", b=shape[1], c=shape[2])
        elif len(shape) == 4:
            ap = ap.rearrange("p (a b c d) -> p a b c d", b=shape[1], c=shape[2], d=shape[3])
        t = T(name, ap)
        ov = []
        keep = []
        for (gs, ge, gt) in self.ghosts:
            if gs < e and s < ge:
                ov.append(gt)
                if not (s <= gs and ge <= e):
                    keep.append((gs, ge, gt))
            else:
                keep.append((gs, ge, gt))
        self.ghosts = keep
        t.overlaps = ov
        self.live.append((s, e, t))
        return t

    def mark(self):
        return (self.top, len(self.live))

    def release(self, m):
        top, nl = m
        for g in self.live[nl:]:
            self.ghosts.append(g)
        self.live = self.live[:nl]
        self.top = top


def bc(ap_tensor, offset, pairs):
    return AP(ap_tensor, offset, [list(p) for p in pairs])


def pack_weights(w_in, wba, wbr, w_out, w_fin, w_fout):
    ws = np.zeros((NB, 128, 4096), np.float32)

    def kc_block(W, c0, ncols):
        blk = np.zeros((128, 8, 512), np.float32)
        blk[:, :, :ncols] = W[:, c0:c0 + ncols].reshape(8, 128, ncols).transpose(1, 0, 2)
        return blk.reshape(128, 4096)
    ws[0] = kc_block(w_in, 0, 512)
    ws[1] = kc_block(w_in, 512, 256)
    for i in range(3):
        ws[2 + i] = kc_block(w_in, 768 + 512 * i, 512)
    ws[5] = kc_block(w_in, 768 + 1536, 256)
    for i in range(4):
        ws[6 + i] = kc_block(w_in, 2560 + 512 * i, 512)
    for i in range(2):
        blk = np.zeros((128, 8, 512), np.float32)
        blk[0:64] = wba[:, i * 512:(i + 1) * 512].reshape(8, 64, 512).transpose(1, 0, 2)
        ws[10 + i] = blk.reshape(128, 4096)
    ws[12] = wbr.reshape(4, 128, 1024).transpose(1, 0, 2).reshape(128, 4096)
    for i in range(2):
        ws[13 + i] = kc_block(w_out, 512 * i, 512)
    for i in range(11):
        blk = np.zeros((128, 8, 512), np.float32)
        Wr = w_fin.reshape(8, 128, 2 * DFF).transpose(1, 0, 2)
        blk[:, :, 0:256] = Wr[:, :, i * 256:(i + 1) * 256]
        blk[:, :, 256:512] = Wr[:, :, DFF + i * 256:DFF + (i + 1) * 256]
        ws[15 + i] = blk.reshape(128, 4096)
    for hf in range(2):
        for gi in range(3):
            ncc = 8 if gi < 2 else 6
            blk = np.zeros((128, 8, 512), np.float32)
            rows = w_fout[gi * 1024:gi * 1024 + ncc * 128, hf * 512:(hf + 1) * 512]
            blk[:, :ncc, :] = rows.reshape(ncc, 128, 512).transpose(1, 0, 2)
            ws[26 + hf * 3 + gi] = blk.reshape(128, 4096)
    return ws


def fm(v, nch):
    return np.ascontiguousarray(v.reshape(nch, 128).T)


def col_layout():
    names = [("g_attn", 8), ("g_ffn", 8), ("mu", 14), ("bgate", 16), ("w0", 4), ("a0", 4), ("kk", 4), ("ka", 4),
             ("rk", 4), ("lnw", 4), ("lnb", 4), ("cb", 22), ("cw0", 22), ("cw1", 22), ("cw2", 22)]
    off = {}
    o = 0
    for n, w in names:
        off[n] = (o, w)
        o += w
    return off, o


COLS, NCOLS = col_layout()


def const_layout():
    names = [("ident", 128), ("maskA", 256), ("maskA0", 256), ("maskS", 256),
             ("mUs", 128), ("mUi", 128), ("mUs2", 128), ("mUi2", 128), ("mLs", 128),
             ("smUs", 128), ("smUi", 128), ("smUs2", 128), ("smUi2", 128), ("smLs", 128),
             ("blockm", 128), ("seqm", 16), ("lastm", 16), ("segP", 256), ("segS", 128), ("gfin", 1024), ("sink", 8), ("sinkS", 2)]
    off = {}
    o = 0
    for n, w in names:
        off[n] = (o, w)
        o += w
    return off, o


CONSTS, NCONST = const_layout()


def make_consts(norm_final, sinks):
    c = np.zeros((128, NCONST), np.float32)

    def put(n, a):
        o, w = CONSTS[n]
        c[:, o:o + w] = a
    i = np.arange(128)[:, None]
    cc = np.arange(256)[None, :]
    dist = 128 + i - cc
    band = (dist >= 0) & (dist <= 128)
    put("ident", np.eye(128, dtype=np.float32))
    put("maskA", np.where(band, 0.0, NEG))
    put("maskA0", np.where(band & (cc >= 128), 0.0, NEG))
    ms = np.full((128, 256), NEG, np.float32)
    ms[0:32] = np.tile(np.where(band, 0.0, NEG)[0:8], (4, 1))
    put("maskS", ms)
    s = np.arange(128)[:, None]
    t = np.arange(128)[None, :]
    mus = (s < t).astype(np.float32)
    mui = (s <= t).astype(np.float32)
    mls = (t < s).astype(np.float32)
    put("mUs", mus); put("mUi", mui); put("mUs2", mus); put("mUi2", mui); put("mLs", mls)
    same = ((s // 8) == (t // 8)).astype(np.float32)
    put("smUs", mus * same); put("smUi", mui * same); put("smUs2", mus * same); put("smUi2", mui * same)
    put("smLs", mls * same)
    put("blockm", ((s // 64) == (t // 64)).astype(np.float32))
    sq = np.arange(16)[None, :]
    put("seqm", ((s // 8) == sq).astype(np.float32))
    put("lastm", (s == sq * 8 + 7).astype(np.float32))
    tt = np.arange(256)[None, :]
    put("segP", np.broadcast_to((tt % 128 != 0).astype(np.float32), (128, 256)))
    t8 = np.arange(128)[None, :]
    put("segS", np.broadcast_to((t8 % 8 != 0).astype(np.float32), (128, 128)))
    put("gfin", np.broadcast_to(norm_final[None, :], (128, 1024)))
    put("sink", np.broadcast_to(sinks[None, :], (128, 8)))
    sS = np.zeros((128, 2), np.float32)
    for kvh in range(2):
        for g in range(4):
            sS[g * 8:(g + 1) * 8, kvh] = sinks[kvh * 4 + g]
    put("sinkS", sS)
    return c


def build(cfg):
    NSEQ = cfg["nseq"]
    SEQ = cfg["seq"]
    TP = cfg["tp"]
    HAS_S = cfg.get("sample", True)
    DBG = cfg.get("dbg", None)
    nmt = SEQ // TP

    nc = bass.Bass("TRN2", target_bir_lowering=False)
    P = Prog(nc)
    P.psum_init()
    A = Arena(P, cfg.get("arena_words", 53000))

    xp_d = P.dram("xp", [NSEQ, SEQ, D], F32, "ExternalInput")
    ws_d = P.dram("wstream", [NB, 128, 4096], F32, "ExternalInput")
    cols_d = P.dram("cols", [128, NCOLS], F32, "ExternalInput")
    const_d = P.dram("consts", [128, NCONST], F32, "ExternalInput")
    lora_d = P.dram("lora", [128, 1024], F32, "ExternalInput")
    yp_d = P.dram("yp", [NSEQ, SEQ, D], F32, "ExternalOutput")
    wkp_d = P.dram("wkp", [NSEQ, 128, 128], F32, "ExternalOutput")
    wvp_d = P.dram("wvp", [NSEQ, 128, 128], F32, "ExternalOutput")
    shp_d = P.dram("shp", [NSEQ, RIN], F32, "ExternalOutput")
    wkvp_d = P.dram("wkvp", [NSEQ, 8, 64, 64], F32, "ExternalOutput")
    cvp_d = P.dram("cvp", [NSEQ, 2, DFF], F32, "ExternalOutput")
    if HAS_S:
        xs_d = P.dram("xsm", [128, D], F32, "ExternalInput")
        ck_d = P.dram("ck", [16, 128, 128], F32, "ExternalInput")
        cv_d = P.dram("cv", [16, 128, 128], F32, "ExternalInput")
        ssh_d = P.dram("ssh", [16, RIN], F32, "ExternalInput")
        swkv_d = P.dram("swkv", [16, 8, 64, 64], F32, "ExternalInput")
        scv_d = P.dram("scv", [32, DFF], F32, "ExternalInput")
        ys_d = P.dram("ys", [128, D], F32, "ExternalOutput")
        wks_d = P.dram("wks", [16, 128, 128], F32, "ExternalOutput")
        wvs_d = P.dram("wvs", [16, 128, 128], F32, "ExternalOutput")
        shs_d = P.dram("shs", [16, RIN], F32, "ExternalOutput")
        wkvs_d = P.dram("wkvs", [16, 8, 64, 64], F32, "ExternalOutput")
        cvs_d = P.dram("cvs", [32, DFF], F32, "ExternalOutput")
    wbf = [P.dram("wbf%d" % g, [4, 128, 4096], BF16) for g in range(8)]
    outs = []
    dbg_out = {}

    def tt(eng, out, in0, in1, op, R, W):
        return P.op(eng, lambda e: e.tensor_tensor(out=out, in0=in0, in1=in1, op=op), R, W)

    def ts(eng, out, in0, s1, s2, op0, op1, R, W):
        if s2 is None:
            return P.op(eng, lambda e: e.tensor_scalar(out=out, in0=in0, scalar1=s1, scalar2=None, op0=op0), R, W)
        return P.op(eng, lambda e: e.tensor_scalar(out=out, in0=in0, scalar1=s1, scalar2=s2, op0=op0, op1=op1), R, W)

    def stt(eng, out, in0, sc, in1, op0, op1, R, W):
        return P.op(eng, lambda e: e.scalar_tensor_tensor(out=out, in0=in0, scalar=sc, in1=in1, op0=op0, op1=op1), R, W)

    def act(out, in_, func, R, W, bias=None, scale=None, accum=None):
        kw = {}
        if bias is not None:
            kw["bias"] = bias
        if scale is not None:
            kw["scale"] = scale
        if accum is not None:
            kw["accum_out"] = accum
        return P.op("act", lambda e: e.activation(out=out, in_=in_, func=func, **kw), R, W)

    def cp(eng, out, in_, R, W):
        if eng == "act":
            return P.op("act", lambda e: e.copy(out=out, in_=in_), R, W)
        return P.op(eng, lambda e: e.tensor_copy(out=out, in_=in_), R, W)

    def mm(out, lhsT, rhs, start, stop, R, W):
        return P.op("pe", lambda e: e.matmul(out, lhsT, rhs, start=start, stop=stop, skip_group_check=True), R, W)

    def tr(out, in_, ident, R, W):
        return P.op("pe", lambda e: e.transpose(out, in_, ident), R, W)

    def red(eng, out, in_, op, R, W):
        return P.op(eng, lambda e: e.tensor_reduce(out=out, in_=in_, axis=AX.X, op=op), R, W)

    def recip(out, in_, R, W):
        return P.op("dve", lambda e: e.reciprocal(out=out, in_=in_), R, W)

    def mset(eng, ap, v, W):
        return P.op(eng, lambda e: e.memset(ap, v), [], W)

    def dump(name, t, ap, shape):
        if DBG and name in DBG and name not in dbg_out:
            d = P.dram("dbg_" + name, list(shape), ap.dtype, "ExternalOutput")
            P.dma("sp", d.ap, ap, t, reads=[t], writes=[d], final=True)
            dbg_out[name] = d
            outs.append(d)

    cst = A.alloc("cst", [NCONST])
    colt = A.alloc("colt", [NCOLS])
    omm = A.alloc("omm", [14])
    omka = A.alloc("omka", [4])
    nsink = A.alloc("nsink", [8])
    identb = A.alloc("identb", [128], BF16)
    blockb = A.alloc("blockb", [128], BF16)
    lorab = A.alloc("lorab", [1024], BF16)
    ones64 = A.alloc("ones64", [64])
    NSLOT = 4
    slots = [A.alloc("wslot%d" % i, [4096], BF16) for i in range(NSLOT)]
    xts = [A.alloc("xt%d" % i, [TP // 128, D]) for i in range(2)]
    Sf = A.alloc("Sf", [4, 128])
    Sb = A.alloc("Sb", [4, 128], BF16)
    shc = A.alloc("shc", [14])
    cvc = A.alloc("cvc", [NFC, 2])
    kT = A.alloc("kT", [2, 128 + TP], BF16)
    vall = A.alloc("vall", [1 + TP // 128, 128], BF16)
    ystg = [A.alloc("ystg%d" % i, [D]) for i in range(1)]

    def C(name):
        o, w = CONSTS[name]
        return cst[:, o:o + w]

    def CL(name, c=None):
        o, w = COLS[name]
        if c is None:
            return colt[:, o:o + w]
        return colt[:, o + c:o + c + 1]

    P.dma("sp", cst[:], const_d.ap, cst, writes=[cst])
    P.dma("sp", colt[:], cols_d.ap, colt, writes=[colt])
    m0 = A.mark()
    lstage = A.alloc("lstage", [1024])
    P.dma("sp", lstage[:], lora_d.ap, lstage, writes=[lstage])
    cp("act", lorab[:], lstage[:], [lstage], [lorab])
    A.release(m0)
    cp("dve", identb[:], C("ident"), [cst], [identb])
    cp("dve", blockb[:], C("blockm"), [cst], [blockb])
    mset("pool", ones64[:], 1.0, [ones64])
    o_mu, _ = COLS["mu"]
    ts("dve", omm[:], colt[:, o_mu:o_mu + 14], -1.0, 1.0, ALU.mult, ALU.add, [colt], [omm])
    o_ka, _ = COLS["ka"]
    ts("dve", omka[:], colt[:, o_ka:o_ka + 4], -1.0, 1.0, ALU.mult, ALU.add, [colt], [omka])
    ts("dve", nsink[:], C("sink"), -1.0, None, ALU.mult, None, [cst], [nsink])

    for b in range(NB):
        g = b // 4
        P.dma("pool", wbf[g].ap[b % 4], ws_d.ap[b], wbf[g], reads=[], writes=[wbf[g]])

    n_macro = NSEQ * nmt + (1 if HAS_S else 0)
    total_uses = n_macro * NB
    wstate = {"issued": 0, "used": 0}

    def w_issue():
        i = wstate["issued"]
        if i >= total_uses:
            return
        b = i % NB
        s = slots[i % NSLOT]
        P.dma("sp", s[:], wbf[b // 4].ap[b % 4], s, reads=[wbf[b // 4]], writes=[s])
        wstate["issued"] += 1

    def w_next(b):
        i = wstate["used"]
        assert i % NB == b, (i, b)
        while wstate["issued"] < min(total_uses, i + NSLOT - 1):
            w_issue()
        wstate["used"] += 1
        return slots[i % NSLOT]

    STOP = cfg.get("stop", None)
    import os as _os
    SKIP = _os.environ.get("KSKIP", "")
    OQ = cfg.get("outq", "sp")

    def chk(name):
        if STOP == name:
            raise StopBuild()

    def macro(kind, si, mi, xbuf):
        T_ = TP if kind == "p" else 128
        NS = T_ // 128
        nseg, L = (1, T_) if kind == "p" else (16, 8)
        first = (mi == 0)
        last = (kind == "s") or (mi == nmt - 1)
        xt = xts[xbuf]
        pre = "s" if kind == "s" else ""
        mk = A.mark()

        def norm_T(gname, hT):
            mkn = A.mark()
            ss = A.alloc("ss", [4])
            junk = A.alloc("junk", [D], BF16)
            for n in range(NS):
                act(junk[:], xt[:, n, :], AF.Square, [xt], [junk, ss], accum=ss[:, n:n + 1])
            ts("dve", ss[:, 0:NS], ss[:, 0:NS], 1.0 / D, NORM_EPS, ALU.mult, ALU.add, [ss], [ss])
            act(ss[:, 0:NS], ss[:, 0:NS], AF.Sqrt, [ss], [ss])
            recip(ss[:, 0:NS], ss[:, 0:NS], [ss], [ss])
            o_g, _ = COLS[gname]
            for n in range(NS):
                xn = A.alloc("xn", [D], BF16)
                ts("dve", xn[:], xt[:, n, :], ss[:, n:n + 1], None, ALU.mult, None, [xt, ss], [xn])
                ps = P.psum()
                psb = ps[:].bitcast(BF16)
                for c in range(8):
                    tr(psb[:, c * 128:(c + 1) * 128], xn[:, c * 128:(c + 1) * 128], identb[:], [xn, identb], [ps])
                gb = bc(colt.h.tensor, colt[:, o_g:o_g + 8].offset, [colt[:].ap[0], [1, 8], [0, 128]])
                tt("dve", hT[:, :, n * 128:(n + 1) * 128], psb.rearrange("p (c t) -> p c t", t=128), gb, ALU.mult,
                   [ps, colt], [hT])
                P.pfree(ps)
            A.release(mkn)

        qT = A.alloc("qT", [8, T_], BF16)
        gates = A.alloc("gates", [16, T_], BF16)
        buf = A.alloc("buf", [14, nseg, L + 1])
        oT = A.alloc("oT", [8, T_], BF16)
        orw = A.alloc("orw", [4, T_], BF16)
        kvst = A.alloc("kvst", [256])
        if kind == "s":
            vnb = A.alloc("vnb", [16, 128], BF16)
        mkh = A.mark()
        hT = A.alloc("hT", [8, T_], BF16)
        norm_T("g_attn", hT)
        dump(pre + "hT", hT, hT[:], [128, 8, T_])
        chk("norm")

        blk = w_next(0)
        bv = blk[:].rearrange("p (k c) -> p k c", c=512)
        for h in range(8):
            ps = P.psum()
            for kc in range(8):
                mm(ps[0:64, 0:T_], bv[:, kc, h * 64:(h + 1) * 64], hT[:, kc, :], kc == 0, kc == 7, [blk, hT], [ps])
            cp("act", qT[0:64, h, :], ps[0:64, 0:T_], [ps], [qT])
            P.pfree(ps)
        dump(pre + "qT0", qT, qT[0:64], [64, 8, T_])
        chk("q")
        blk = w_next(1)
        bv = blk[:].rearrange("p (k c) -> p k c", c=512)
        for kvh in range(2):
            ps = P.psum()
            for kc in range(8):
                mm(ps[0:64, 0:T_], bv[:, kc, kvh * 64:(kvh + 1) * 64], hT[:, kc, :], kc == 0, kc == 7, [blk, hT], [ps])
            cp("act", kT[0:64, kvh, 128:128 + T_], ps[0:64, 0:T_], [ps], [kT])
            P.pfree(ps)
        dump(pre + "kT", kT, kT[0:64], [64, 2, 128 + TP])
        chk("kv1")
        if kind == "p":
            for n in range(NS):
                ps = P.psum()
                for kc in range(8):
                    mm(ps[:, 0:256], hT[:, kc, n * 128:(n + 1) * 128], bv[:, kc, 0:256], kc == 0, kc == 7, [blk, hT], [ps])
                cp("act", vall[:, 1 + n, :], ps[:, 128:256], [ps], [vall])
                if STOP == "kv2":
                    P.pfree(ps)
                    continue
                if last and n == NS - 1:
                    cp("act", kvst[:], ps[:, 0:256], [ps], [kvst])
                    dump(pre + "kvst", kvst, kvst[:], [128, 256])
                    if STOP == "kv3":
                        P.pfree(ps)
                        continue
                    P.dma(OQ, wkp_d.ap[si], kvst[:, 0:128], kvst, reads=[kvst], writes=[wkp_d], final=True)
                    P.dma(OQ, wvp_d.ap[si], kvst[:, 128:256], kvst, reads=[kvst], writes=[wvp_d], final=True)
                P.pfree(ps)
        else:
            mkv = A.mark()
            kvn = A.alloc("kvn", [16, 256])
            for s in range(16):
                ps = P.psum()
                for kc in range(8):
                    mm(ps[0:8, 0:256], hT[:, kc, s * 8:(s + 1) * 8], bv[:, kc, 0:256], kc == 0, kc == 7, [blk, hT], [ps])
                cp("act", kvn[0:8, s, :], ps[0:8, 0:256], [ps], [kvn])
                P.pfree(ps)
            cp("dve", vnb[0:8, :, :], kvn[0:8, :, 128:256], [kvn], [vnb])
            P.dma(OQ, wks_d.ap[:, 0:120, :], ck_d.ap[:, 8:128, :], wks_d, reads=[], writes=[wks_d], final=True)
            P.dma(OQ, wvs_d.ap[:, 0:120, :], cv_d.ap[:, 8:128, :], wvs_d, reads=[], writes=[wvs_d], final=True)
            P.dma(OQ, wks_d.ap[:, 120:128, :].rearrange("s t f -> t s f"), kvn[0:8, :, 0:128], kvn, reads=[kvn], writes=[wks_d], final=True)
            P.dma(OQ, wvs_d.ap[:, 120:128, :].rearrange("s t f -> t s f"), kvn[0:8, :, 128:256], kvn, reads=[kvn], writes=[wvs_d], final=True)
            A.release(mkv)
        dump(pre + "vall", vall, vall[:], [128, 1 + TP // 128, 128])
        chk("kv2")
        chk("kv3")
        chk("kv")
        for c in range(14):
            if c % 4 == 0:
                blk = w_next(2 + c // 4)
                bv = blk[:].rearrange("p (k c) -> p k c", c=512)
            ps = P.psum()
            for kc in range(8):
                mm(ps[:, 0:T_], bv[:, kc, (c % 4) * 128:(c % 4 + 1) * 128], hT[:, kc, :], kc == 0, kc == 7, [blk, hT], [ps])
            cp("act", buf[:, c, :, 1:L + 1], ps[:, 0:T_].rearrange("p (s l) -> p s l", l=L), [ps], [buf])
            P.pfree(ps)
        chk("rw")
        for c in range(16):
            if c % 4 == 0:
                blk = w_next(6 + c // 4)
                bv = blk[:].rearrange("p (k c) -> p k c", c=512)
            ps = P.psum()
            for kc in range(8):
                mm(ps[:, 0:T_], bv[:, kc, (c % 4) * 128:(c % 4 + 1) * 128], hT[:, kc, :], kc == 0, kc == 7, [blk, hT], [ps])
            act(gates[:, c, :], ps[:, 0:T_], AF.Sigmoid, [ps, colt], [gates], bias=CL("bgate", c))
            P.pfree(ps)
        dump(pre + "qT", qT, qT[0:64], [64, 8, T_])
        dump(pre + "gates", gates, gates[:], [128, 16, T_])
        chk("win")
        A.release(mkh)

        if kind == "p":
            if first:
                mset("pool", buf[:, :, 0, 0:1], 0.0, [buf])
            else:
                cp("pool", buf[:, :, 0, 0:1], shc[:].rearrange("p (c o) -> p c o", o=1), [shc], [buf])
        else:
            mks = A.mark()
            sst = A.alloc("sst", [RIN])
            P.dma("pool", sst[0:16, :], ssh_d.ap, sst, writes=[sst])
            for g in range(4):
                ps = P.psum()
                ncg = 4 if g < 3 else 2
                for j in range(ncg):
                    c = g * 4 + j
                    tr(ps[:, j * 16:(j + 1) * 16], sst[0:16, c * 128:(c + 1) * 128], C("ident")[0:16, 0:16], [sst, cst], [ps])
                cp("dve", buf[:, g * 4:g * 4 + ncg, :, 0], ps[:, 0:ncg * 16].rearrange("p (c s) -> p c s", s=16), [ps], [buf])
                P.pfree(ps)
            A.release(mks)
        dump(pre + "buf", buf, buf[:], [128, 14, nseg, L + 1])

        if last:
            mksh = A.mark()
            shst = A.alloc("shst", [RIN])
            for g in range(4):
                ps = P.psum()
                ncg = 4 if g < 3 else 2
                for j in range(ncg):
                    c = g * 4 + j
                    tr(ps[0:nseg, j * 128:(j + 1) * 128], buf[:, c, :, L], C("ident"), [buf, cst], [ps])
                cp("act", shst[0:nseg, g * 512:g * 512 + ncg * 128], ps[0:nseg, 0:ncg * 128], [ps], [shst])
                P.pfree(ps)
            if kind == "p":
                P.dma(OQ, shp_d.ap[si:si + 1, :], shst[0:1, :], shst, reads=[shst], writes=[shp_d], final=True)
            else:
                P.dma(OQ, shs_d.ap, shst[0:16, :], shst, reads=[shst], writes=[shs_d], final=True)
            A.release(mksh)
        elif kind == "p":
            cp("pool", shc[:].rearrange("p (c o) -> p c o", o=1), buf[:, :, 0, L:L + 1], [buf], [shc])

        AR = A.alloc("AR", [4, NS, 2, 128], BF16)
        Bt = A.alloc("Bt", [4, T_], BF16)
        Kt = A.alloc("Kt", [4, T_], BF16)
        bhtok = A.alloc("bhtok", [NS, 512], BF16)
        khtok = A.alloc("khtok", [NS, 512], BF16)
        vtok = A.alloc("vtok", [NS, 512], BF16)
        gT = A.alloc("gT", [4, T_])
        bon = A.alloc("bon", [4, T_])
        E1 = A.alloc("E1", [4, T_])
        lin = A.alloc("lin", [2, T_], BF16)
        mk2 = A.mark()
        xs12 = A.alloc("xs12", [2, nseg, L])
        tA = A.alloc("tA", [2, nseg, L])

        def shiftmix(dst, j, c, tmp, tmp_ap=None):
            tap = tmp[:, j] if tmp_ap is None else tmp_ap
            act(tap, buf[:, c, :, 0:L], AF.Identity, [buf, colt], [tmp], scale=CL("mu", c))
            stt("dve", dst[:, j], buf[:, c, :, 1:L + 1], omm[:, c:c + 1], tap, ALU.mult, ALU.add, [buf, omm, tmp], [dst])

        shiftmix(xs12, 0, 12, tA)
        shiftmix(xs12, 1, 13, tA)
        x12 = xs12[:].rearrange("p c s l -> p c (s l)")
        act(lin[0:64, 0, :], x12[0:64, 0, :], AF.Tanh, [xs12], [lin])
        cp("act", lin[64:128, 0, :], x12[64:128, 0, :], [xs12], [lin])
        act(lin[:, 1, :], x12[:, 1, :], AF.Sigmoid, [xs12], [lin])
        A.release(mk2)

        def attn(M, qap, kprev, kcur, ncur, vprev, vcur, maskap, sinkap, nsinkap, out_ap, in_view, Rq, Rk, Rv, Rs, setidx):
            Wd = 128 + ncur
            s_, p_, pn, pT, st = atmp[setidx]
            ps = P.psum()
            mm(ps[0:M, 0:128], qap, kprev, True, True, Rq + Rk, [ps])
            mm(ps[0:M, 128:Wd], qap, kcur, True, True, Rq + Rk, [ps])
            stt("dve", s_[0:M, 0:Wd], ps[0:M, 0:Wd], 0.125, maskap, ALU.mult, ALU.add, [ps, cst], [s_])
            red("dve", st[0:M, 0:1], s_[0:M, 0:Wd], ALU.max, [s_], [st])
            stt("dve", st[0:M, 1:2], st[0:M, 0:1], -1.0, nsinkap, ALU.mult, ALU.min, [st] + Rs, [st])
            act(p_[0:M, 0:Wd], s_[0:M, 0:Wd], AF.Exp, [s_, st], [p_, st], bias=st[0:M, 1:2], accum=st[0:M, 2:3])
            act(st[0:M, 3:4], sinkap, AF.Exp, [st] + Rs, [st], bias=st[0:M, 1:2])
            tt("dve", st[0:M, 2:3], st[0:M, 2:3], st[0:M, 3:4], ALU.add, [st], [st])
            recip(st[0:M, 2:3], st[0:M, 2:3], [st], [st])
            ts("dve", pn[0:M, 0:Wd], p_[0:M, 0:Wd], st[0:M, 2:3], None, ALU.mult, None, [p_, st], [pn])
            psb = ps[:].bitcast(BF16)
            tr(psb[:, 512:512 + M], pn[0:M, 0:128], identb[0:M, 0:M], [pn, identb], [ps])
            tr(psb[0:ncur, 512 + M:512 + 2 * M], pn[0:M, 128:Wd], identb[0:M, 0:M], [pn, identb], [ps])
            cp("act", pT[:, 0, 0:M], psb[:, 512:512 + M], [ps], [pT])
            cp("act", pT[0:ncur, 1, 0:M], psb[0:ncur, 512 + M:512 + 2 * M], [ps], [pT])
            mm(ps[0:64, 384:384 + M], vprev, pT[:, 0, 0:M], True, False, Rv + [pT], [ps])
            mm(ps[0:64, 384:384 + M], vcur, pT[0:ncur, 1, 0:M], False, True, Rv + [pT], [ps])
            cp("act", out_ap, in_view(ps[0:64, 384:384 + M]), [ps], [oT])
            P.pfree(ps)

        acalls = []
        atmp = [(A.alloc("as%d" % i, [256]), A.alloc("ap%d" % i, [256]), A.alloc("apn%d" % i, [256], BF16),
                 A.alloc("apT%d" % i, [2, 128], BF16), A.alloc("ast%d" % i, [4])) for i in range(4)]

        if kind == "p":
            if first:
                mset("pool", kT[0:64, :, 0:128], 0.0, [kT])
                mset("pool", vall[:, 0, :], 0.0, [vall])
            for n in range(NS):
                mname = "maskA0" if (first and n == 0) else "maskA"
                for h in range(8):
                    kvh = h // 4
                    acalls.append(lambda si_, n=n, h=h, kvh=kvh, mname=mname: attn(
                        128, qT[0:64, h, n * 128:(n + 1) * 128], kT[0:64, kvh, n * 128:n * 128 + 128],
                        kT[0:64, kvh, n * 128 + 128:n * 128 + 256], 128,
                        vall[:, n, kvh * 64:(kvh + 1) * 64], vall[:, n + 1, kvh * 64:(kvh + 1) * 64],
                        C(mname), C("sink")[:, h:h + 1], nsink[:, h:h + 1],
                        oT[0:64, h, n * 128:(n + 1) * 128], lambda a: a, [qT], [kT], [vall], [cst, nsink], si_))
        else:
            vcb = A.alloc("vcb", [16, 128], BF16)
            kcT = A.alloc("kcT", [16, 2, 128], BF16)
            sinkS = A.alloc("sinkS", [4])
            mkk = A.mark()
            kcb = A.alloc("kcb", [16, 128], BF16)
            for q4 in range(4):
                mkc = A.mark()
                stg = A.alloc("cstg", [4, 128])
                P.dma("pool", stg[:], ck_d.ap[q4 * 4:(q4 + 1) * 4].rearrange("s w f -> w s f"), stg, writes=[stg])
                cp("dve", kcb[:, q4 * 4:(q4 + 1) * 4, :], stg[:], [stg], [kcb])
                stg2 = A.alloc("cstg2", [4, 128])
                P.dma("pool", stg2[:], cv_d.ap[q4 * 4:(q4 + 1) * 4].rearrange("s w f -> w s f"), stg2, writes=[stg2])
                cp("dve", vcb[:, q4 * 4:(q4 + 1) * 4, :], stg2[:], [stg2], [vcb])
                A.release(mkc)
            for s in range(16):
                ps = P.psum()
                psb = ps[:].bitcast(BF16)
                tr(psb[:, 0:128], kcb[:, s, :], identb[:], [kcb, identb], [ps])
                cp("act", kcT[0:64, s, 0, :], psb[0:64, 0:128], [ps], [kcT])
                cp("act", kcT[0:64, s, 1, :], psb[64:128, 0:128], [ps], [kcT])
                P.pfree(ps)
            A.release(mkk)
            cp("pool", sinkS[:, 0:2], C("sinkS"), [cst], [sinkS])
            ts("pool", sinkS[:, 2:4], C("sinkS"), -1.0, None, ALU.mult, None, [cst], [sinkS])
            qS = A.alloc("qS", [16, 2, 32], BF16)
            for kvh in range(2):
                cp("pool", qS[0:64, :, kvh, :].rearrange("p s (g t) -> p s g t", t=8),
                   qT[0:64, kvh * 4:(kvh + 1) * 4, :].rearrange("p g (s t) -> p s g t", t=8), [qT], [qS])
            for s in range(16):
                for kvh in range(2):
                    acalls.append(lambda si_, s=s, kvh=kvh: attn(
                        32, qS[0:64, s, kvh, :], kcT[0:64, s, kvh, :],
                        kT[0:64, kvh, 128 + s * 8:128 + (s + 1) * 8], 8,
                        vcb[:, s, kvh * 64:(kvh + 1) * 64], vnb[0:8, s, kvh * 64:(kvh + 1) * 64],
                        C("maskS")[0:32, 0:136], sinkS[0:32, kvh:kvh + 1], sinkS[0:32, 2 + kvh:3 + kvh],
                        oT[0:64, kvh * 4:(kvh + 1) * 4, s * 8:(s + 1) * 8],
                        lambda a: a.rearrange("p (g t) -> p g t", t=8), [qS], [kT, kcT], [vcb, vnb], [sinkS], si_))

        segm = C("segP")[:, 0:T_] if kind == "p" else C("segS")
        nsg, Ls = (NS, 128) if kind == "p" else (16, 8)
        def prep_c(c):
            xr = A.alloc("xr", [3, nseg, L])
            xrf = xr[:].rearrange("p c s l -> p c (s l)")
            r_, k_, v_ = xrf[:, 0, :], xrf[:, 1, :], xrf[:, 2, :]
            lw = A.alloc("lw", [T_])
            asg = A.alloc("asg", [T_])
            kkn = A.alloc("kkn", [T_])
            kmod = A.alloc("kmod", [T_])
            t1 = A.alloc("t1", [T_])
            t2 = A.alloc("t2", [T_])
            lP = A.alloc("lP", [T_])
            E2 = A.alloc("E2", [T_])
            E3 = A.alloc("E3", [T_])
            Dd = A.alloc("Dd", [T_])
            sqb = A.alloc("sqb", [T_], BF16)
            bhT = A.alloc("bhT", [T_], BF16)
            khT = A.alloc("khT", [T_], BF16)
            vTb = A.alloc("vTb", [T_], BF16)
            for j, tq in enumerate((t1, t2, Dd)):
                shiftmix(xr, j, c + 4 * j, tq, tq[:].rearrange("p (s l) -> p s l", l=L))
            ps = P.psum()
            mm(ps[:, 0:T_], lorab[0:64, c * 128:(c + 1) * 128], lin[0:64, 0, :], True, True, [lorab, lin], [ps])
            act(lw[:], ps[:, 0:T_], AF.Sigmoid, [ps, colt], [lw], bias=CL("w0", c))
            P.pfree(ps)
            ps = P.psum()
            mm(ps[:, 0:T_], lorab[64:128, c * 128:(c + 1) * 128], lin[64:128, 0, :], True, True, [lorab, lin], [ps])
            act(asg[:], ps[:, 0:T_], AF.Sigmoid, [ps, colt], [asg], bias=CL("a0", c))
            P.pfree(ps)
            ps = P.psum()
            mm(ps[:, 0:T_], lorab[:, 512 + c * 128:512 + (c + 1) * 128], lin[:, 1, :], True, True, [lorab, lin], [ps])
            cp("act", gT[:, c, :], ps[:, 0:T_], [ps], [gT])
            P.pfree(ps)
            ts("dve", kkn[:], k_, CL("kk", c), None, ALU.mult, None, [xr, colt], [kkn])
            act(sqb[:], kkn[:], AF.Square, [kkn], [sqb])
            ps = P.psum()
            mm(ps[:, 0:T_], blockb[:], sqb[:], True, True, [blockb, sqb], [ps])
            act(t1[:], ps[:, 0:T_], AF.Sqrt, [ps], [t1])
            P.pfree(ps)
            ts("dve", t1[:], t1[:], 1e-12, None, ALU.max, None, [t1], [t1])
            recip(t1[:], t1[:], [t1], [t1])
            tt("dve", kkn[:], kkn[:], t1[:], ALU.mult, [kkn, t1], [kkn])
            act(t2[:], asg[:], AF.Identity, [asg, colt, omka], [t2], bias=omka[:, c:c + 1], scale=CL("ka", c))
            tt("pool", kmod[:], k_, t2[:], ALU.mult, [xr, t2], [kmod])
            tt("pool", t2[:], r_, kmod[:], ALU.mult, [xr, kmod], [t2])
            ts("dve", sqb[:], t2[:], CL("rk", c), None, ALU.mult, None, [t2, colt], [sqb])
            ps = P.psum()
            mm(ps[:, 0:T_], blockb[:], sqb[:], True, True, [blockb, sqb], [ps])
            tt("dve", bon[:, c, :], ps[:, 0:T_], v_, ALU.mult, [ps, xr], [bon])
            P.pfree(ps)
            P.op("dve", lambda e, o=lP[:], d0=segm, d1=lw[:]: e.tensor_tensor_scan(out=o, data0=d0, data1=d1, initial=0.0,
                                                                                  op0=ALU.mult, op1=ALU.add),
                 [cst, lw], [lP])
            act(E1[:, c, :], lP[:], AF.Exp, [lP], [E1], scale=-0.6065306597126334)
            act(E2[:], lP[:], AF.Exp, [lP], [E2], scale=0.6065306597126334)
            tt("pool", t1[:], lP[:], lw[:], ALU.subtract, [lP, lw], [t1])
            act(E3[:], t1[:], AF.Exp, [t1], [E3], scale=-0.6065306597126334)
            lP3 = lP[:].rearrange("p (g l) -> p g l", l=Ls)
            lastc = bc(lP.h.tensor, lP3[:, :, Ls - 1:Ls].offset, [lP[:].ap[0], [Ls, nsg], [0, Ls]])
            tt("dve", t2[:].rearrange("p (g l) -> p g l", l=Ls), lastc, lP3, ALU.subtract, [lP], [t2])
            act(Dd[:], t2[:], AF.Exp, [t2], [Dd], scale=-0.6065306597126334)
            stt("dve", AR[:, c, :, 0, :], kkn[:].rearrange("p (n t) -> p n t", t=128), -1.0,
                E3[:].rearrange("p (n t) -> p n t", t=128), ALU.mult, ALU.mult, [kkn, E3], [AR])
            tt("pool", AR[:, c, :, 1, :], r_.rearrange("p (n t) -> p n t", t=128),
               E1[:, c, :].rearrange("p (n t) -> p n t", t=128), ALU.mult, [xr, E1], [AR])
            tt("pool", t1[:], kkn[:], asg[:], ALU.mult, [kkn, asg], [t1])
            tt("pool", Bt[:, c, :], t1[:], E2[:], ALU.mult, [t1, E2], [Bt])
            tt("dve", Kt[:, c, :], kmod[:], E2[:], ALU.mult, [kmod, E2], [Kt])
            tt("pool", bhT[:], t1[:], Dd[:], ALU.mult, [t1, Dd], [bhT])
            tt("dve", khT[:], kmod[:], Dd[:], ALU.mult, [kmod, Dd], [khT])
            cp("act", vTb[:], v_, [xr], [vTb])
            for n in range(NS):
                ps = P.psum()
                psb = ps[:].bitcast(BF16)
                tr(psb[:, 0:128], bhT[:, n * 128:(n + 1) * 128], identb[:], [bhT, identb], [ps])
                tr(psb[:, 128:256], khT[:, n * 128:(n + 1) * 128], identb[:], [khT, identb], [ps])
                tr(psb[:, 256:384], vTb[:, n * 128:(n + 1) * 128], identb[:], [vTb, identb], [ps])
                cp("act", bhtok[:, n, c * 128:(c + 1) * 128], psb[:, 0:128], [ps], [bhtok])
                cp("act", khtok[:, n, c * 128:(c + 1) * 128], psb[:, 128:256], [ps], [khtok])
                cp("dve", vtok[:, n, c * 128:(c + 1) * 128], psb[:, 256:384], [ps], [vtok])
                P.pfree(ps)
            if c == 0:
                dump(pre + "lw0", lw, lw[:], [128, T_])
                dump(pre + "asg0", asg, asg[:], [128, T_])
                dump(pre + "kkn0", kkn, kkn[:], [128, T_])
                dump(pre + "kmod0", kmod, kmod[:], [128, T_])
                dump(pre + "lP0", lP, lP[:], [128, T_])

        mp = "" if kind == "p" else "s"
        mUs, mLs = C(mp + "mUs"), C(mp + "mLs")
        o3, _ = CONSTS[mp + "mUi"]
        mask3 = cst[:, o3:o3 + 384]
        PQ = [[A.alloc("PQ%d_%d" % (n, c), [2, 2, 128], BF16) for c in range(4)] for n in range(NS)]
        ACC = [[A.alloc("ACC%d_%d" % (n, c), [2, 2, 128], BF16) for c in range(4)] for n in range(NS)]
        M3 = [[A.alloc("M3_%d_%d" % (n, h), [384], BF16) for h in range(8)] for n in range(NS)]
        ib = bc(identb.h.tensor, identb[:].offset, [identb[:].ap[0], [0, 2], [1, 128]])

        def inv_task(n, c):
            PQc, ACCc = PQ[n][c], ACC[n][c]
            for h2 in range(2):
                h = 2 * c + h2
                b0 = h2 * 64
                psG = P.psum()
                arv = AR[b0:b0 + 64, c, n, :, :].rearrange("p a t -> p (a t)")
                mm(psG[:, 0:256], Bt[b0:b0 + 64, c, n * 128:(n + 1) * 128], arv, True, True, [Bt, AR], [psG])
                mm(psG[:, 256:512], Kt[b0:b0 + 64, c, n * 128:(n + 1) * 128], arv, True, True, [Kt, AR], [psG])
                tt("dve", PQc[:, h2, 0, :], psG[:, 0:128], mUs, ALU.mult, [psG, cst], [PQc])
                tt("dve", M3[n][h][:], psG[:, 128:512], mask3, ALU.mult, [psG, cst], [M3[n][h]])
                P.pfree(psG)
                psL = P.psum()
                mm(psL[:, 0:128], AR[b0:b0 + 64, c, n, 0, :], Bt[b0:b0 + 64, c, n * 128:(n + 1) * 128],
                   True, True, [AR, Bt], [psL])
                tt("dve", PQc[:, h2, 1, :], psL[:, 0:128], mLs, ALU.mult, [psL, cst], [PQc])
                P.pfree(psL)
            tt("dve", ACCc[:, :, 0, :], PQc[:, :, 0, :], ib, ALU.add, [PQc, identb], [ACCc])
            tt("dve", ACCc[:, :, 1, :], PQc[:, :, 1, :], ib, ALU.add, [PQc, identb], [ACCc])
            for lvl in range(1, 7):
                lastl = (lvl == 6)
                ps = P.psum()
                for h2 in range(2):
                    mm(ps[:, h2 * 256:h2 * 256 + 128], PQc[:, h2, 1, :], PQc[:, h2, 0, :], True, True, [PQc], [ps])
                    if not lastl:
                        mm(ps[:, h2 * 256 + 128:h2 * 256 + 256], PQc[:, h2, 0, :], PQc[:, h2, 1, :], True, True, [PQc], [ps])
                if lastl:
                    cp("act", PQc[:, :, 0, :], ps[:].rearrange("p (h a t) -> p h a t", a=2, t=128)[:, :, 0, :], [ps], [PQc])
                else:
                    cp("act", PQc[:].rearrange("p h a t -> p (h a t)"), ps[:], [ps], [PQc])
                P.pfree(ps)
                ps = P.psum()
                for h2 in range(2):
                    mm(ps[:, h2 * 256:h2 * 256 + 128], ACCc[:, h2, 1, :], PQc[:, h2, 0, :], True, True, [ACCc, PQc], [ps])
                    if not lastl:
                        mm(ps[:, h2 * 256 + 128:h2 * 256 + 256], PQc[:, h2, 0, :], ACCc[:, h2, 1, :], True, True, [ACCc, PQc], [ps])
                if lastl:
                    tt("dve", ACCc[:, :, 0, :], ps[:].rearrange("p (h a t) -> p h a t", a=2, t=128)[:, :, 0, :],
                       ACCc[:, :, 0, :], ALU.add, [ps, ACCc], [ACCc])
                else:
                    tt("dve", ACCc[:].rearrange("p h a t -> p (h a t)"), ps[:], ACCc[:].rearrange("p h a t -> p (h a t)"),
                       ALU.add, [ps, ACCc], [ACCc])
                P.pfree(ps)

        half = len(acalls) // 2
        allb = list(P.psum_all)
        for pi, c0_ in enumerate((0, 2)):
            mk3 = A.mark()
            tasks = []
            assert len(P.psum_free) == 8
            if pi == 0:
                pool_prep, pool_att, pool_inv, natt = allb[0:4], allb[4:8], [], 4
            else:
                pool_prep, pool_att, pool_inv, natt = allb[0:2], allb[2:4], allb[4:8], 2
            for k2, c in enumerate((c0_, c0_ + 1)):
                if k2 == 0:
                    P.psum_free = list(pool_prep)
                elif len(pool_prep) == 2:
                    P.psum_free = [pool_prep[1], pool_prep[0]]
                P.begin_task()
                prep_c(c)
                tasks.append(P.end_task())
            P.psum_free = list(pool_att)
            calls = acalls[pi * half:(pi + 1) * half]
            per = len(calls) // natt
            for j in range(natt):
                P.begin_task()
                for k in range(per):
                    want = pool_att[(j + k) % len(pool_att)]
                    P.psum_free.remove(want)
                    P.psum_free.insert(0, want)
                    calls[j * per + k](j)
                tasks.append(P.end_task())
            if pool_inv:
                P.psum_free = list(pool_inv)
                for n in range(NS):
                    for c in (0, 1):
                        P.psum_free.append(P.psum_free.pop(0))
                        P.begin_task()
                        inv_task(n, c)
                        tasks.append(P.end_task())
            P.psum_free = list(allb)
            P.interleave(tasks)
            A.release(mk3)
        if kind == "p" and not last:
            cp("pool", kT[0:64, :, 0:128], kT[0:64, :, T_:T_ + 128], [kT], [kT])
            cp("pool", vall[:, 0, :], vall[:, NS, :], [vall], [vall])
        dump(pre + "oT", oT, oT[0:64], [64, 8, T_])
        chk("attn")
        dump(pre + "bon", bon, bon[:], [128, 4, T_])
        dump(pre + "gT", gT, gT[:], [128, 4, T_])
        chk("prep")

        if kind == "p" and first:
            mset("pool", Sf[:], 0.0, [Sf])
            mset("pool", Sb[:], 0.0, [Sb])
        mkscan = A.mark()
        yo = A.alloc("yo", [4, T_])
        Xs = A.alloc("Xs", [512], BF16)
        Us = A.alloc("Us", [512], BF16)
        tmpS = A.alloc("tmpS", [4, 128])
        ysb = A.alloc("ysb", [512])
        ydd = A.alloc("ydd", [512])
        yst = A.alloc("yst", [16])
        if kind == "s":
            X0 = A.alloc("X0", [512])
            Y0 = A.alloc("Y0", [512])
            E1tok = A.alloc("E1tok", [512])

        assert len(P.psum_free) == 8
        tasks = []
        for n in range(NS):
            for c in (2, 3):
                P.psum_free.append(P.psum_free.pop(0))
                P.begin_task()
                inv_task(n, c)
                tasks.append(P.end_task())
        P.interleave(tasks)
        chk("sc1")

        for n in range(NS):
            chk("sc2")
            if kind == "s":
                sample_s0_terms(AR, X0, Y0, E1, E1tok)
            psX = P.psum()
            for c in range(4):
                if kind == "p":
                    mm(psX[:, c * 128:(c + 1) * 128], AR[:, c, n, 0, :], Sb[:, c, :], True, False, [AR, Sb], [psX])
                for h2 in range(2):
                    h = 2 * c + h2
                    mm(psX[:, h * 64:(h + 1) * 64], M3[n][h][:, 128:256], vtok[:, n, h * 64:(h + 1) * 64], kind == "s", True,
                       [M3[n][h], vtok], [psX])
            if kind == "p":
                cp("act", Xs[:], psX[:], [psX], [Xs])
            else:
                tt("dve", Xs[:], psX[:], X0[:], ALU.add, [psX, X0], [Xs])
            P.pfree(psX)
            psU = P.psum()
            for h in range(8):
                mm(psU[:, h * 64:(h + 1) * 64], ACC[n][h // 2][:, h % 2, 0, :], Xs[:, h * 64:(h + 1) * 64], True, True,
                   [ACC[n][h // 2], Xs], [psU])
            cp("act", Us[:], psU[:], [psU], [Us])
            P.pfree(psU)
            if n == 0:
                dump(pre + "Us", Us, Us[:], [128, 512])
            chk("sc3")
            psY = P.psum()
            for c in range(4):
                if kind == "p":
                    mm(psY[:, c * 128:(c + 1) * 128], AR[:, c, n, 1, :], Sb[:, c, :], True, False, [AR, Sb], [psY])
                for h2 in range(2):
                    h = 2 * c + h2
                    mm(psY[:, h * 64:(h + 1) * 64], M3[n][h][:, 0:128], Us[:, h * 64:(h + 1) * 64], kind == "s", False, [M3[n][h], Us], [psY])
                    mm(psY[:, h * 64:(h + 1) * 64], M3[n][h][:, 256:384], vtok[:, n, h * 64:(h + 1) * 64], False, True, [M3[n][h], vtok], [psY])
            if kind == "p":
                psS = P.psum()
                for c in range(4):
                    mm(psS[:, c * 128:(c + 1) * 128], bhtok[:, n, c * 128:(c + 1) * 128], Us[:, c * 128:(c + 1) * 128], True, False,
                       [bhtok, Us], [psS])
                    mm(psS[:, c * 128:(c + 1) * 128], khtok[:, n, c * 128:(c + 1) * 128], vtok[:, n, c * 128:(c + 1) * 128], False, True,
                       [khtok, vtok], [psS])
                bmb = bc(cst.h.tensor, C("blockm").offset, [cst[:].ap[0], [0, 4], [1, 128]])
                tt("dve", tmpS[:], psS[:].rearrange("p (c t) -> p c t", t=128), bmb, ALU.mult, [psS, cst], [tmpS])
                P.pfree(psS)
                pcb = bc(E1.h.tensor, E1[:, 0, n * 128 + 127:n * 128 + 128].offset, [E1[:].ap[0], [T_, 4], [0, 128]])
                tt("dve", Sf[:], Sf[:], pcb, ALU.mult, [Sf, E1], [Sf])
                tt("dve", Sf[:], Sf[:], tmpS[:], ALU.add, [Sf, tmpS], [Sf])
                cp("act", Sb[:], Sf[:], [Sf], [Sb])
            else:
                sample_state_out(Us, vtok, bhtok, khtok, E1tok)
            chk("sc4")
            if kind == "s":
                tt("dve", ysb[:], psY[:], Y0[:], ALU.add, [psY, Y0], [ysb])
                ysrc, ysrcT = ysb[:], ysb
            else:
                cp("act", ysb[:], psY[:], [psY], [ysb])
                ysrc, ysrcT = ysb[:], ysb
            P.pfree(psY)
            if n == 0:
                dump(pre + "ysb", ysb, ysb[:], [128, 512])
            y3 = ysrc.rearrange("p (h i) -> p h i", i=64)
            red("dve", yst[:, 0:8], y3, ALU.add, [ysrcT], [yst])
            ts("dve", yst[:, 0:8], yst[:, 0:8], -1.0 / 64, None, ALU.mult, None, [yst], [yst])
            nmb = bc(yst.h.tensor, yst[:, 0:8].offset, [yst[:].ap[0], [1, 8], [0, 64]])
            d3 = ydd[:].rearrange("p (h i) -> p h i", i=64)
            tt("dve", d3, y3, nmb, ALU.add, [ysrcT, yst], [ydd])
            tt("pool", ysb[:], ydd[:], ydd[:], ALU.mult, [ydd], [ysb])
            red("dve", yst[:, 8:16], ysb[:].rearrange("p (h i) -> p h i", i=64), ALU.add, [ysb], [yst])
            ts("dve", yst[:, 8:16], yst[:, 8:16], 1.0 / 64, LNX_EPS, ALU.mult, ALU.add, [yst], [yst])
            act(yst[:, 8:16], yst[:, 8:16], AF.Sqrt, [yst], [yst])
            recip(yst[:, 8:16], yst[:, 8:16], [yst], [yst])
            rsb = bc(yst.h.tensor, yst[:, 8:16].offset, [yst[:].ap[0], [1, 8], [0, 64]])
            tt("dve", d3, d3, rsb, ALU.mult, [ydd, yst], [ydd])
            ps = P.psum()
            for c in range(4):
                tr(ps[:, c * 128:(c + 1) * 128], ydd[:, c * 128:(c + 1) * 128], C("ident"), [ydd, cst], [ps])
            for c in range(4):
                act(yo[:, c, n * 128:(n + 1) * 128], ps[:, c * 128:(c + 1) * 128], AF.Identity, [ps, colt], [yo],
                    bias=CL("lnb", c), scale=CL("lnw", c))
            P.pfree(ps)
        tt("dve", yo[:], yo[:], bon[:], ALU.add, [yo, bon], [yo])
        tt("dve", orw[:], yo[:], gT[:], ALU.mult, [yo, gT], [orw])
        dump(pre + "orw", orw, orw[:], [128, 4, T_])
        chk("scan")
        if kind == "p" and last:
            sst2 = A.alloc("sst2", [8, 64])
            ps = P.psum()
            for c in range(4):
                tr(ps[:, c * 128:(c + 1) * 128], Sf[:, c, :], C("ident"), [Sf, cst], [ps])
            for c in range(4):
                for h2 in range(2):
                    cp("act", sst2[0:64, 2 * c + h2, :], ps[h2 * 64:(h2 + 1) * 64, c * 128 + h2 * 64:c * 128 + h2 * 64 + 64],
                       [ps], [sst2])
            P.pfree(ps)
            P.dma(OQ, wkvp_d.ap[si].rearrange("h i j -> i h j"), sst2[0:64, :, :], sst2, reads=[sst2], writes=[wkvp_d], final=True)

        A.release(mkscan)
        merged = A.alloc("merged", [8, T_], BF16)
        mt1 = A.alloc("mt1", [T_])
        mt2 = A.alloc("mt2", [T_])
        baT = A.alloc("baT", [8, T_])
        for hf in range(2):
            wblk = w_next(10 + hf)
            wv = wblk[:].rearrange("p (h c) -> p h c", c=512)
            for cc in range(4):
                c = hf * 4 + cc
                ps = P.psum()
                for h in range(8):
                    mm(ps[:, 0:T_], wv[0:64, h, cc * 128:(cc + 1) * 128], oT[0:64, h, :], h == 0, h == 7, [wblk, oT], [ps])
                tt("dve", baT[:, c, :], ps[:, 0:T_], gates[:, c, :], ALU.mult, [ps, gates], [baT])
                P.pfree(ps)
        wblk = w_next(12)
        wv = wblk[:].rearrange("p (k c) -> p k c", c=1024)
        for c in range(8):
            ps = P.psum()
            for k4 in range(4):
                mm(ps[:, 0:T_], wv[:, k4, c * 128:(c + 1) * 128], orw[:, k4, :], k4 == 0, k4 == 3, [wblk, orw], [ps])
            tt("dve", mt1[:], ps[:, 0:T_], gates[:, 8 + c, :], ALU.mult, [ps, gates], [mt1])
            P.pfree(ps)
            tt("pool", merged[:, c, :], mt1[:], baT[:, c, :], ALU.add, [mt1, baT], [merged])
        dump(pre + "merged", merged, merged[:], [128, 8, T_])
        for hf in range(2):
            wblk = w_next(13 + hf)
            wv = wblk[:].rearrange("p (k c) -> p k c", c=512)
            for n in range(NS):
                ps = P.psum()
                for kc in range(8):
                    mm(ps[:], merged[:, kc, n * 128:(n + 1) * 128], wv[:, kc, :], kc == 0, kc == 7, [wblk, merged], [ps])
                tt("dve", xt[:, n, hf * 512:(hf + 1) * 512], xt[:, n, hf * 512:(hf + 1) * 512], ps[:], ALU.add, [xt, ps], [xt])
                P.pfree(ps)
        dump(pre + "x1", xt, xt[:, 0:NS, :], [128, NS, D])
        chk("merge")
        A.release(mk)

        mk = A.mark()
        hT = A.alloc("h2T", [8, T_], BF16)
        norm_T("g_ffn", hT)
        actT = A.alloc("actT", [NFC, T_], BF16)
        if kind == "p":
            if first:
                mset("pool", cvc[:], 0.0, [cvc])
            carry = cvc
        else:
            carry = None
            cst_s = A.alloc("cst_s", [DFF])
            P.dma("pool", cst_s[0:32, :], scv_d.ap, cst_s, writes=[cst_s])
            cvs = A.alloc("cvs", [NFC, 16, 2])
            for g in range(6):
                ncg = 4 if g < 5 else 2
                ps = P.psum()
                for j in range(ncg):
                    cc = g * 4 + j
                    tr(ps[:, j * 32:(j + 1) * 32], cst_s[0:32, cc * 128:(cc + 1) * 128], C("ident")[0:32, 0:32], [cst_s, cst], [ps])
                cp("dve", cvs[:, g * 4:g * 4 + ncg].rearrange("p c s j -> p c (s j)"), ps[:, 0:ncg * 32].rearrange("p (c x) -> p c x", x=32),
                   [ps], [cvs])
                P.pfree(ps)
            cvo = A.alloc("cvo", [NFC, 16, 2])
        NFS = 3
        gbs = [A.alloc("gbuf%d" % i, [nseg, L + 2]) for i in range(NFS)]
        fcs = [[A.alloc("fc%d_%d" % (i, k), [nseg, L]) for k in range(3)] for i in range(NFS)]
        def ffn_cc(cc, wblk):
            wv = wblk[:].rearrange("p (k c) -> p k c", c=512)
            j = cc % 2
            psg = P.psum()
            psv = P.psum()
            for kc in range(8):
                mm(psg[:, 0:T_], wv[:, kc, j * 128:(j + 1) * 128], hT[:, kc, :], kc == 0, kc == 7, [wblk, hT], [psg])
            for kc in range(8):
                mm(psv[:, 0:T_], wv[:, kc, 256 + j * 128:256 + (j + 1) * 128], hT[:, kc, :], kc == 0, kc == 7, [wblk, hT], [psv])
            g3 = psg[:, 0:T_].rearrange("p (s l) -> p s l", l=L)
            slot = cc % NFS
            gbT = gbs[slot]
            gq = gbT[:]
            f0, f1, f2 = fcs[slot]
            c0, c1, c2 = f0[:], f1[:], f2[:]
            if kind == "p":
                cp("pool", gq[:, 0, 0:2], cvc[:, cc, :], [cvc], [gbT])
            else:
                cp("pool", gq[:, :, 0:2], cvs[:, cc, :, :], [cvs], [gbT])
            cp("act", gq[:, :, 2:L + 2], g3, [psg], [gbT])
            act(c0, g3, AF.Identity, [psg, colt], [f0], bias=CL("cb", cc), scale=CL("cw2", cc))
            stt("dve", c1, gq[:, :, 1:L + 1], CL("cw1", cc), c0, ALU.mult, ALU.add, [gbT, colt, f0], [f1])
            stt("dve", c0, gq[:, :, 0:L], CL("cw0", cc), c1, ALU.mult, ALU.add, [gbT, colt, f1], [f0])
            if kind == "p":
                cp("pool", cvc[:, cc, :], gq[:, 0, L:L + 2], [gbT], [cvc])
            else:
                cp("pool", cvo[:, cc, :, :], gq[:, :, L:L + 2], [gbT], [cvo])
            act(c1, c0, AF.Square, [f0], [f1], scale=0.26712319)
            stt("dve", c1, c1, 1.5957691216057308, c0, ALU.add, ALU.mult, [f1, f0], [f1])
            act(c2, c1, AF.Sigmoid, [f1], [f2])
            tt("pool", c2, c2, c0, ALU.mult, [f2, f0], [f2])
            tt("dve", actT[:, cc, :].rearrange("p (s l) -> p s l", l=L), c2, psv[:, 0:T_].rearrange("p (s l) -> p s l", l=L), ALU.mult,
               [f2, psv], [actT])
            P.pfree(psg)
            P.pfree(psv)

        blkmap = {}
        FG = int(_os.environ.get("FFG", "3"))
        for g0 in range(0, NFC, FG):
            grp = list(range(g0, min(NFC, g0 + FG)))
            for cc in grp:
                if cc // 2 not in blkmap:
                    blkmap[cc // 2] = w_next(15 + cc // 2)
            tasks = []
            for cc in grp:
                P.begin_task()
                ffn_cc(cc, blkmap[cc // 2])
                tasks.append(P.end_task())
            P.interleave(tasks)
        dump(pre + "actT", actT, actT[:], [128, NFC, T_])
        chk("ffn")
        if last:
            cvst = A.alloc("cvst", [DFF])
            nr = 2 if kind == "p" else 32
            for g in range(6):
                ncg = 4 if g < 5 else 2
                ps = P.psum()
                for j in range(ncg):
                    cc = g * 4 + j
                    src = cvc[:, cc, :] if kind == "p" else cvo[:, cc].rearrange("p s j -> p (s j)")
                    tr(ps[0:nr, j * 128:(j + 1) * 128], src, C("ident"), [cvc if kind == "p" else cvo, cst], [ps])
                cp("act", cvst[0:nr, g * 512:g * 512 + ncg * 128], ps[0:nr, 0:ncg * 128], [ps], [cvst])
                P.pfree(ps)
            if kind == "p":
                P.dma(OQ, cvp_d.ap[si], cvst[0:2, :], cvst, reads=[cvst], writes=[cvp_d], final=True)
            else:
                P.dma(OQ, cvs_d.ap, cvst[0:32, :], cvst, reads=[cvst], writes=[cvs_d], final=True)
        for hf in range(2):
            pss = [P.psum() for _ in range(NS)]
            for gi in range(3):
                wblk = w_next(26 + hf * 3 + gi)
                wv = wblk[:].rearrange("p (k c) -> p k c", c=512)
                ncc = 8 if gi < 2 else 6
                for n in range(NS):
                    for k in range(ncc):
                        cc = gi * 8 + k
                        mm(pss[n][:], actT[:, cc, n * 128:(n + 1) * 128], wv[:, k, :], cc == 0, cc == NFC - 1, [wblk, actT], [pss[n]])
            for n in range(NS):
                tt("dve", xt[:, n, hf * 512:(hf + 1) * 512], xt[:, n, hf * 512:(hf + 1) * 512], pss[n][:], ALU.add, [xt, pss[n]], [xt])
                P.pfree(pss[n])
        ss2 = A.alloc("ss2", [4])
        junk2 = A.alloc("junk2", [D], BF16)
        for n in range(NS):
            act(junk2[:], xt[:, n, :], AF.Square, [xt], [junk2, ss2], accum=ss2[:, n:n + 1])
        ts("dve", ss2[:, 0:NS], ss2[:, 0:NS], 1.0 / D, NORM_EPS, ALU.mult, ALU.add, [ss2], [ss2])
        act(ss2[:, 0:NS], ss2[:, 0:NS], AF.Sqrt, [ss2], [ss2])
        recip(ss2[:, 0:NS], ss2[:, 0:NS], [ss2], [ss2])
        for n in range(NS):
            yb = ystg[0]
            stt("dve", yb[:], xt[:, n, :], ss2[:, n:n + 1], C("gfin"), ALU.mult, ALU.mult, [xt, ss2, cst], [yb])
            if kind == "p":
                t0 = mi * TP + n * 128
                P.dma(OQ, yp_d.ap[si, t0:t0 + 128, :], yb[:], yb, reads=[yb], writes=[yp_d], final=True)
            else:
                P.dma(OQ, ys_d.ap, yb[:], yb, reads=[yb], writes=[ys_d], final=True)
        A.release(mk)

    sstate = {}

    def sample_s0_terms(AR, X0, Y0, E1, E1tok):
        ps = P.psum()
        for c in range(4):
            tr(ps[:, c * 128:(c + 1) * 128], E1[:, c, 0:128], C("ident"), [E1, cst], [ps])
        cp("act", E1tok[:], ps[:], [ps], [E1tok])
        P.pfree(ps)
        smb = bc(cst.h.tensor, C("seqm").offset, [cst[:].ap[0], [1, 16], [0, 64]])
        for h in range(8):
            c, h2 = h // 2, h % 2
            b0 = h2 * 64
            mk_h = A.mark()
            S0h = A.alloc("S0h", [16, 64])
            S0T = A.alloc("S0T", [16, 64], BF16)
            P.dma("pool", S0h[0:64, :, :], swkv_d.ap[:, h, :, :].rearrange("s i j -> i s j"), S0h, writes=[S0h])
            for q in range(2):
                ps = P.psum()
                for s8 in range(8):
                    s = q * 8 + s8
                    tr(ps[0:64, s8 * 64:(s8 + 1) * 64], S0h[0:64, s, :], C("ident")[0:64, 0:64], [S0h, cst], [ps])
                cp("act", S0T[b0:b0 + 64, q * 8:(q + 1) * 8, :], ps[0:64, :].rearrange("p (s i) -> p s i", i=64), [ps], [S0T])
                P.pfree(ps)
            for which, dst in ((0, X0), (1, Y0)):
                mk_ = A.mark()
                tmp = A.alloc("s0tmp", [16, 64])
                for q in range(2):
                    ps = P.psum()
                    mm(ps[:], AR[b0:b0 + 64, c, 0, which, :], S0T[b0:b0 + 64, q * 8:(q + 1) * 8, :].rearrange("p s i -> p (s i)"),
                       True, True, [AR, S0T], [ps])
                    smq = bc(cst.h.tensor, C("seqm")[:, q * 8:q * 8 + 8].offset, [cst[:].ap[0], [1, 8], [0, 64]])
                    tt("dve", tmp[:, q * 8:(q + 1) * 8, :], ps[:].rearrange("p (s i) -> p s i", i=64), smq, ALU.mult, [ps, cst], [tmp])
                    P.pfree(ps)
                red("dve", dst[:, h * 64:(h + 1) * 64], tmp[:].rearrange("p s i -> p i s"), ALU.add, [tmp], [dst])
                A.release(mk_)
            A.release(mk_h)

    def sample_state_out(Us, vtok, bhtok, khtok, E1tok):
        for h in range(8):
            mk_ = A.mark()
            S0h = A.alloc("S0h2", [16, 64])
            P.dma("pool", S0h[0:64, :, :], swkv_d.ap[:, h, :, :].rearrange("s i j -> i s j"), S0h, writes=[S0h])
            bhm = A.alloc("bhm", [16, 64], BF16)
            khm = A.alloc("khm", [16, 64], BF16)
            Wm = A.alloc("Wm", [16, 64])
            So = A.alloc("So", [16, 64])

            def b2(t, col0):
                return bc(t.h.tensor, t[:, 0, col0:col0 + 64].offset, [t[:].ap[0], [0, 16], [1, 64]])
            smb = bc(cst.h.tensor, C("seqm").offset, [cst[:].ap[0], [1, 16], [0, 64]])
            lmb = bc(cst.h.tensor, C("lastm").offset, [cst[:].ap[0], [1, 16], [0, 64]])
            tt("dve", bhm[:], b2(bhtok, h * 64), smb, ALU.mult, [bhtok, cst], [bhm])
            tt("dve", khm[:], b2(khtok, h * 64), smb, ALU.mult, [khtok, cst], [khm])
            e1b = bc(E1tok.h.tensor, E1tok[:, h * 64:(h + 1) * 64].offset, [E1tok[:].ap[0], [0, 16], [1, 64]])
            tt("dve", Wm[:], e1b, lmb, ALU.mult, [E1tok, cst], [Wm])
            for q in range(2):
                psA = P.psum()
                psP = P.psum()
                mm(psA[0:64, :], Us[:, h * 64:(h + 1) * 64], bhm[:, q * 8:(q + 1) * 8, :].rearrange("p s j -> p (s j)"), True, False, [Us, bhm], [psA])
                mm(psA[0:64, :], vtok[:, 0, h * 64:(h + 1) * 64], khm[:, q * 8:(q + 1) * 8, :].rearrange("p s j -> p (s j)"), False, True, [vtok, khm], [psA])
                mm(psP[0:64, :], ones64[:, 0:64], Wm[:, q * 8:(q + 1) * 8, :].rearrange("p s j -> p (s j)"), True, True, [ones64, Wm], [psP])
                so = So[0:64, q * 8:(q + 1) * 8, :].rearrange("p s j -> p (s j)")
                tt("dve", so, S0h[0:64, q * 8:(q + 1) * 8, :].rearrange("p s j -> p (s j)"), psP[0:64, :], ALU.mult, [S0h, psP], [So])
                tt("dve", so, so, psA[0:64, :], ALU.add, [So, psA], [So])
                P.pfree(psA)
                P.pfree(psP)
            P.dma(OQ, wkvs_d.ap[:, h, :, :].rearrange("s i j -> i s j"), So[0:64, :, :], So, reads=[So], writes=[wkvs_d], final=True)
            A.release(mk_)

    xi = 0
    sched = [("p", si, mi) for si in range(NSEQ) for mi in range(nmt)]
    if HAS_S:
        sched.append(("s", 0, 0))

    def load_x(k, xbuf):
        kind, si, mi = sched[k]
        xt = xts[xbuf]
        if kind == "p":
            for n in range(TP // 128):
                t0 = mi * TP + n * 128
                P.dma("sp", xt[:, n, :], xp_d.ap[si, t0:t0 + 128, :], xt, reads=[], writes=[xt])
        else:
            P.dma("sp", xt[:, 0, :], xs_d.ap, xt, reads=[], writes=[xt])

    load_x(0, 0)
    for k in range(len(sched)):
        if k + 1 < len(sched):
            load_x(k + 1, (k + 1) % 2)
        kind, si, mi = sched[k]
        try:
            macro(kind, si, mi, k % 2)
        except StopBuild:
            break

    P.emit(list(P.outdeps))
    stats = {e: P.n[e] for e in ENGS}
    stats["sems"] = P.nsem
    stats["arena_peak_bytes"] = A.peak * 4
    return nc, stats, list(dbg_out.keys())


def prep_shared(inp):
    l = 0
    ws = pack_weights(inp["w_in"][l], inp["w_branch_attn"][l], inp["w_branch_rwkv"][l], inp["w_out"][l],
                      inp["w_ffn_in"][l], inp["w_ffn_out"][l])
    cols = np.zeros((128, NCOLS), np.float32)

    def put(n, a):
        o, w = COLS[n]
        cols[:, o:o + w] = a
    put("g_attn", fm(inp["norm_attn"][l], 8)); put("g_ffn", fm(inp["norm_ffn"][l], 8))
    put("mu", fm(inp["mu_shift"][l], 14)); put("bgate", fm(inp["b_gate"][l], 16))
    put("w0", fm(inp["w0"][l], 4)); put("a0", fm(inp["a0"][l], 4)); put("kk", fm(inp["k_k"][l], 4))
    put("ka", fm(inp["k_a"][l], 4)); put("rk", fm(inp["r_k"][l].reshape(-1), 4))
    put("lnw", fm(inp["lnx_w"][l], 4)); put("lnb", fm(inp["lnx_b"][l], 4))
    put("cb", fm(inp["ffn_conv_b"][l], NFC))
    for j in range(3):
        put("cw%d" % j, fm(inp["ffn_conv_w"][l][j], NFC))
    consts = make_consts(np.asarray(inp["norm_final"]), np.asarray(inp["attn_sinks"][l]))
    lora = np.zeros((128, 1024), np.float32)
    lora[0:64, 0:512] = inp["w_lora_up"][l]
    lora[64:128, 0:512] = inp["a_lora_up"][l]
    lora[:, 512:1024] = inp["g_lora_up"][l]
    return {"wstream": ws, "cols": cols, "consts": consts, "lora": lora}


_CACHE = {}


def run(inp, n_cores, nseq, seq, tp, sample=True, dbg=None, stop=None):
    inp = {k: np.asarray(v) for k, v in inp.items()}
    cfg = dict(nseq=nseq, seq=seq, tp=tp, sample=sample, dbg=dbg, stop=stop)
    key = (nseq, seq, tp, sample, tuple(dbg) if dbg else None, stop)
    if key not in _CACHE:
        _CACHE[key] = build(cfg)
    nc, stats, dbgn = _CACHE[key]
    shared = prep_shared(inp)
    in_maps = []
    for c in range(n_cores):
        m = dict(shared)
        m["xp"] = np.ascontiguousarray(inp["x_prompt"][c * nseq:(c + 1) * nseq])
        if sample:
            sl = slice(c * 16, (c + 1) * 16)
            m["xsm"] = np.ascontiguousarray(inp["x_sample"][sl].reshape(128, D))
            m["ck"] = np.ascontiguousarray(inp["cache_win_k"][0, sl].reshape(16, 128, 128))
            m["cv"] = np.ascontiguousarray(inp["cache_win_v"][0, sl].reshape(16, 128, 128))
            m["ssh"] = np.ascontiguousarray(inp["state_shift"][0, sl])
            m["swkv"] = np.ascontiguousarray(inp["state_wkv"][0, sl])
            m["scv"] = np.ascontiguousarray(inp["state_ffn_conv"][0, sl].reshape(32, DFF))
        in_maps.append(m)
    res = run_bass_kernel_spmd(nc, in_maps, core_ids=list(range(n_cores)))
    R = res.results
    cat = lambda n: np.concatenate([r[n] for r in R], axis=0)
    B = n_cores * nseq
    out = [cat("yp"),
           None,
           cat("wkp").reshape(1, B, 128, 2, 64), cat("wvp").reshape(1, B, 128, 2, 64),
           cat("shp").reshape(1, B, RIN), cat("wkvp").reshape(1, B, 8, 64, 64), cat("cvp").reshape(1, B, 2, DFF)]
    if sample:
        S = n_cores * 16
        out[1] = cat("ys").reshape(S, 8, D)
        out += [cat("wks").reshape(1, S, 128, 2, 64), cat("wvs").reshape(1, S, 128, 2, 64),
                cat("shs").reshape(1, S, RIN), cat("wkvs").reshape(1, S, 8, 64, 64), cat("cvs").reshape(1, S, 2, DFF)]
    dbgv = {n: [r["dbg_" + n] for r in R] for n in dbgn}
    return tuple(out), dbgv, stats


def kernel(**inputs):
    out, _, _ = run(inputs, 8, 2, 2048, 256, True, None)
    return tuple(np.ascontiguousarray(o, dtype=np.float32) for o in out)
```
